# Optimizing a Trainium2 kernel written in Bass

```python
import math
import jax, jax.numpy as jnp
from jax import lax
import numpy as np

D_MODEL = 2048
BATCH = 4
SEQ = 4096
DEPTH = 2

GROUP_WIDTH = D_MODEL // 4
MIX_WIDTH = 4 * GROUP_WIDTH
D_FF = 4 * D_MODEL
S5_CH = 16
S5_GROUPS = GROUP_WIDTH // S5_CH
S5_STATE = 64
S5_DT_MIN = 1e-3
S5_DT_MAX = 1e-1
RET_HEADS = 4
RET_QK = 64
RET_V = GROUP_WIDTH // RET_HEADS
RET_CHUNK = 128
SWA_HD = 64
SWA_HEADS = GROUP_WIDTH // SWA_HD
SWA_KV_HEADS = 2
WINDOW = 128
MLA_HEADS = 4
MLA_Q_RANK = 384
MLA_KV_RANK = 128
MLA_NOPE = 128
MLA_ROPE = 64
MLA_V = GROUP_WIDTH // MLA_HEADS
MLA_BLOCK = 128
ROPE_BASE = 10000.0
EPS = 1e-6
NEG = -1e30

IN_SPLITS = (
    GROUP_WIDTH,
    RET_HEADS * RET_QK,
    RET_HEADS * RET_QK,
    GROUP_WIDTH,
    GROUP_WIDTH,
    SWA_HEADS * SWA_HD,
    SWA_KV_HEADS * SWA_HD,
    SWA_KV_HEADS * SWA_HD,
    MLA_Q_RANK,
    MLA_KV_RANK,
    MLA_ROPE,
)
N_IN = sum(IN_SPLITS)
SPLIT_IDX = [int(v) for v in np.cumsum(IN_SPLITS)[:-1]]

kernel_name = 'hybrid_parallel_head_group_block'


def rms_norm(x, gain=None):
    xf = x.astype(jnp.float32)
    y = xf * lax.rsqrt(jnp.mean(xf * xf, axis=-1, keepdims=True) + EPS)
    if gain is not None:
        y = y * gain.astype(jnp.float32)
    return y.astype(x.dtype)


def rotary(x):
    seq, d = x.shape[1], x.shape[-1]
    inv = ROPE_BASE ** (-jnp.arange(0, d, 2, dtype=jnp.float32) / d)
    ang = jnp.arange(seq, dtype=jnp.float32)[:, None] * inv[None, :]
    cos = jnp.cos(ang)[None, :, None, :].astype(x.dtype)
    sin = jnp.sin(ang)[None, :, None, :].astype(x.dtype)
    x1, x2 = x[..., :d // 2], x[..., d // 2:]
    return jnp.concatenate([x1 * cos - x2 * sin, x1 * sin + x2 * cos], axis=-1)


def s5_mixer(u, lam_re, lam_im, log_dt, b_re, b_im, c_re, c_im, d_skip, glu_w, glu_b):
    f32 = jnp.float32
    bsz, seq, _ = u.shape
    uf = u.astype(f32).reshape(bsz, seq, S5_GROUPS, S5_CH)
    dt = jnp.exp(log_dt.astype(f32))[:, None]
    lr, li = lam_re.astype(f32), lam_im.astype(f32)
    mag = jnp.exp(lr * dt)
    ar, ai = mag * jnp.cos(li * dt), mag * jnp.sin(li * dt)
    den = lr * lr + li * li
    cr = ((ar - 1.0) * lr + ai * li) / den
    ci = (ai * lr - (ar - 1.0) * li) / den
    br, bi = b_re.astype(f32), b_im.astype(f32)
    bbar_r = cr[..., None] * br - ci[..., None] * bi
    bbar_i = cr[..., None] * bi + ci[..., None] * br
    bu_r = jnp.einsum('bsgh,gph->bsgp', uf, bbar_r)
    bu_i = jnp.einsum('bsgh,gph->bsgp', uf, bbar_i)
    a_r = jnp.broadcast_to(ar, bu_r.shape)
    a_i = jnp.broadcast_to(ai, bu_i.shape)

    def combine(e1, e2):
        a1r, a1i, b1r, b1i = e1
        a2r, a2i, b2r, b2i = e2
        return (a2r * a1r - a2i * a1i, a2r * a1i + a2i * a1r,
                a2r * b1r - a2i * b1i + b2r, a2r * b1i + a2i * b1r + b2i)

    _, _, st_r, st_i = lax.associative_scan(combine, (a_r, a_i, bu_r, bu_i), axis=1)
    y = (jnp.einsum('bsgp,ghp->bsgh', st_r, c_re.astype(f32))
         - jnp.einsum('bsgp,ghp->bsgh', st_i, c_im.astype(f32))
         + d_skip.astype(f32) * uf)
    z = jax.nn.gelu(y.reshape(bsz, seq, GROUP_WIDTH)).astype(u.dtype)
    return z * jax.nn.sigmoid(z @ glu_w + glu_b)


def retention_mixer(q, k, v, g):
    f32 = jnp.float32
    bsz, seq = q.shape[:2]
    nck = seq // RET_CHUNK
    q = rotary(q.astype(f32))
    k = rotary(k.astype(f32)) * (RET_QK ** -0.5)
    v = v.astype(f32)
    log_gamma = jnp.log1p(-(2.0 ** (-5.0 - jnp.arange(RET_HEADS, dtype=f32))))
    idx = jnp.arange(RET_CHUNK, dtype=f32)
    rel = idx[:, None] - idx[None, :]
    decay_intra = jnp.where(rel >= 0, jnp.exp(log_gamma[:, None, None] * jnp.maximum(rel, 0.0)), 0.0)
    zeta = jnp.exp(log_gamma[:, None] * (RET_CHUNK - 1.0 - idx))
    xi = jnp.exp(log_gamma[:, None] * (idx + 1.0))
    gamma_chunk = jnp.exp(log_gamma * RET_CHUNK)
    qc = q.reshape(bsz, nck, RET_CHUNK, RET_HEADS, RET_QK)
    kc = k.reshape(bsz, nck, RET_CHUNK, RET_HEADS, RET_QK)
    vc = v.reshape(bsz, nck, RET_CHUNK, RET_HEADS, RET_V)
    s = jnp.einsum('bnchd,bnmhd->bnhcm', qc, kc) * decay_intra
    o_intra = jnp.einsum('bnhcm,bnmhv->bnchv', s, vc)
    kv = jnp.einsum('bnmhd,hm,bnmhv->nbhdv', kc, zeta, vc)

    def step(state, kv_n):
        return gamma_chunk[None, :, None, None] * state + kv_n, state

    _, prev = lax.scan(step, jnp.zeros_like(kv[0]), kv)
    o_cross = jnp.einsum('bnchd,nbhdv->bnchv', qc, prev) * xi.T[None, None, :, :, None]
    o = (o_intra + o_cross).reshape(bsz, seq, RET_HEADS, RET_V)
    o = o * lax.rsqrt(jnp.mean(o * o, axis=-1, keepdims=True) + EPS)
    return o.reshape(bsz, seq, GROUP_WIDTH).astype(g.dtype) * jax.nn.silu(g)


def swa_mixer(q, k, v, sinks):
    f32 = jnp.float32
    bsz, seq = q.shape[:2]
    nb = seq // WINDOW
    grp = SWA_HEADS // SWA_KV_HEADS
    qb = q.reshape(bsz, nb, WINDOW, SWA_KV_HEADS, grp, SWA_HD)
    pad = ((0, 0), (WINDOW, 0), (0, 0), (0, 0))
    kp = jnp.pad(k, pad).reshape(bsz, nb + 1, WINDOW, SWA_KV_HEADS, SWA_HD)
    vp = jnp.pad(v, pad).reshape(bsz, nb + 1, WINDOW, SWA_KV_HEADS, SWA_HD)
    kb = jnp.concatenate([kp[:, :-1], kp[:, 1:]], axis=2)
    vb = jnp.concatenate([vp[:, :-1], vp[:, 1:]], axis=2)
    s = jnp.einsum('bnqkgd,bnjkd->bnkgqj', qb, kb).astype(f32) * (SWA_HD ** -0.5)
    r = jnp.arange(WINDOW)[:, None]
    j = jnp.arange(2 * WINDOW)[None, :]
    dist = r + WINDOW - j
    blk = jnp.arange(nb)[:, None, None]
    valid = (dist >= 0) & (dist < WINDOW) & (blk * WINDOW + j - WINDOW >= 0)
    s = jnp.where(valid[None, :, None, None], s, NEG)
    sink = sinks.astype(f32).reshape(SWA_KV_HEADS, grp)[None, None, :, :, None, None]
    m = jnp.maximum(jnp.max(s, axis=-1, keepdims=True), sink)
    p = jnp.exp(s - m)
    denom = jnp.sum(p, axis=-1, keepdims=True) + jnp.exp(sink - m)
    o = jnp.einsum('bnkgqj,bnjkd->bnqkgd', (p / denom).astype(v.dtype), vb)
    return o.reshape(bsz, seq, GROUP_WIDTH)


def mla_mixer(c_q, c_kv, k_rope, q_norm, kv_norm, w_uq, w_ukv):
    f32 = jnp.float32
    bsz, seq = c_q.shape[:2]
    q = (rms_norm(c_q, q_norm) @ w_uq).reshape(bsz, seq, MLA_HEADS, MLA_NOPE + MLA_ROPE)
    q_nope, q_rope = q[..., :MLA_NOPE], rotary(q[..., MLA_NOPE:])
    kv = (rms_norm(c_kv, kv_norm) @ w_ukv).reshape(bsz, seq, MLA_HEADS, MLA_NOPE + MLA_V)
    k_nope, v = kv[..., :MLA_NOPE], kv[..., MLA_NOPE:]
    k_r = rotary(k_rope[:, :, None, :])[:, :, 0]
    scale = (MLA_NOPE + MLA_ROPE) ** -0.5
    nb = seq // MLA_BLOCK
    k_pos = jnp.arange(seq)

    def block(args):
        i, qn, qr = args
        s = (jnp.einsum('bqhd,bkhd->bhqk', qn, k_nope)
             + jnp.einsum('bqhr,bkr->bhqk', qr, k_r)).astype(f32) * scale
        q_pos = i * MLA_BLOCK + jnp.arange(MLA_BLOCK)
        s = jnp.where(k_pos[None, :] <= q_pos[:, None], s, NEG)
        p = jax.nn.softmax(s, axis=-1).astype(v.dtype)
        return jnp.einsum('bhqk,bkhv->bqhv', p, v)

    qn_b = q_nope.reshape(bsz, nb, MLA_BLOCK, MLA_HEADS, MLA_NOPE).transpose(1, 0, 2, 3, 4)
    qr_b = q_rope.reshape(bsz, nb, MLA_BLOCK, MLA_HEADS, MLA_ROPE).transpose(1, 0, 2, 3, 4)
    o = lax.map(block, (jnp.arange(nb), qn_b, qr_b))
    return o.transpose(1, 0, 2, 3, 4).reshape(bsz, seq, GROUP_WIDTH)


def setup_inputs(seed: int = 0) -> dict:
    key = jax.random.key(seed)
    ks = jax.random.split(key, 32)
    f32 = jnp.float32
    nrm = lambda k, shape, scale: scale * jax.random.normal(k, shape, f32)
    L, D, G, P, H, GW = DEPTH, D_MODEL, S5_GROUPS, S5_STATE, S5_CH, GROUP_WIDTH
    return {
        'x': nrm(ks[0], (BATCH, SEQ, D), 1.0),
        'c': nrm(ks[1], (BATCH, D), 1.0),
        'norm1_g': 1.0 + nrm(ks[2], (L, D), 0.02),
        'norm2_g': 1.0 + nrm(ks[3], (L, D), 0.02),
        'ada_w': nrm(ks[4], (L, D, 6 * D), 0.5 * D ** -0.5),
        'ada_b': nrm(ks[5], (L, 6 * D), 0.02),
        'w_in': nrm(ks[6], (L, D, N_IN), D ** -0.5),
        's5_lambda_re': -0.5 + nrm(ks[7], (L, G, P), 0.01),
        's5_lambda_im': math.pi * jnp.arange(P, dtype=f32) + nrm(ks[8], (L, G, P), 0.01),
        's5_log_dt': jax.random.uniform(ks[9], (L, G), f32, math.log(S5_DT_MIN), math.log(S5_DT_MAX)),
        's5_b_re': nrm(ks[10], (L, G, P, H), (2 * H) ** -0.5),
        's5_b_im': nrm(ks[11], (L, G, P, H), (2 * H) ** -0.5),
        's5_c_re': nrm(ks[12], (L, G, H, P), (2 * P) ** -0.5),
        's5_c_im': nrm(ks[13], (L, G, H, P), (2 * P) ** -0.5),
        's5_d': nrm(ks[14], (L, G, H), 1.0),
        's5_glu_w': nrm(ks[15], (L, GW, GW), GW ** -0.5),
        's5_glu_b': nrm(ks[16], (L, GW), 0.01),
        'swa_sinks': nrm(ks[17], (L, SWA_HEADS), 0.5),
        'mla_q_norm': 1.0 + nrm(ks[18], (L, MLA_Q_RANK), 0.02),
        'mla_kv_norm': 1.0 + nrm(ks[19], (L, MLA_KV_RANK), 0.02),
        'mla_w_uq': nrm(ks[20], (L, MLA_Q_RANK, MLA_HEADS * (MLA_NOPE + MLA_ROPE)), MLA_Q_RANK ** -0.5),
        'mla_w_ukv': nrm(ks[21], (L, MLA_KV_RANK, MLA_HEADS * (MLA_NOPE + MLA_V)), MLA_KV_RANK ** -0.5),
        'w_out': nrm(ks[22], (L, MIX_WIDTH, D), MIX_WIDTH ** -0.5),
        'mlp_w1': nrm(ks[23], (L, D, D_FF), D ** -0.5),
        'mlp_w2': nrm(ks[24], (L, D_FF, D), D_FF ** -0.5),
        'final_norm_g': 1.0 + nrm(ks[25], (D,), 0.02),
    }


def reference(x, c, norm1_g, norm2_g, ada_w, ada_b, w_in, s5_lambda_re, s5_lambda_im, s5_log_dt,
              s5_b_re, s5_b_im, s5_c_re, s5_c_im, s5_d, s5_glu_w, s5_glu_b, swa_sinks,
              mla_q_norm, mla_kv_norm, mla_w_uq, mla_w_ukv, w_out, mlp_w1, mlp_w2, final_norm_g):
    bsz, seq, _ = x.shape
    h = x
    c_act = jax.nn.silu(c)
    for l in range(DEPTH):
        mod = c_act @ ada_w[l] + ada_b[l]
        sh1, sc1, gt1, sh2, sc2, gt2 = [m[:, None, :] for m in jnp.split(mod, 6, axis=-1)]
        a = rms_norm(h, norm1_g[l]) * (1 + sc1) + sh1
        proj = a @ w_in[l]
        (u_s5, r_q, r_k, r_v, r_g, w_q, w_k, w_v,
         m_cq, m_ckv, m_kr) = jnp.split(proj, SPLIT_IDX, axis=-1)
        y_s5 = s5_mixer(u_s5, s5_lambda_re[l], s5_lambda_im[l], s5_log_dt[l], s5_b_re[l], s5_b_im[l],
                        s5_c_re[l], s5_c_im[l], s5_d[l], s5_glu_w[l], s5_glu_b[l])
        y_ret = retention_mixer(r_q.reshape(bsz, seq, RET_HEADS, RET_QK),
                                r_k.reshape(bsz, seq, RET_HEADS, RET_QK),
                                r_v.reshape(bsz, seq, RET_HEADS, RET_V), r_g)
        y_swa = swa_mixer(w_q.reshape(bsz, seq, SWA_HEADS, SWA_HD),
                          w_k.reshape(bsz, seq, SWA_KV_HEADS, SWA_HD),
                          w_v.reshape(bsz, seq, SWA_KV_HEADS, SWA_HD), swa_sinks[l])
        y_mla = mla_mixer(m_cq, m_ckv, m_kr, mla_q_norm[l], mla_kv_norm[l], mla_w_uq[l], mla_w_ukv[l])
        mixed = jnp.concatenate([y_s5, y_ret, y_swa, y_mla], axis=-1) @ w_out[l]
        h = h + gt1 * mixed
        a = rms_norm(h, norm2_g[l]) * (1 + sc2) + sh2
        h = h + gt2 * (jnp.square(jax.nn.relu(a @ mlp_w1[l])) @ mlp_w2[l])
    return rms_norm(h, final_norm_g)
```

```python
import numpy as np
from contextlib import ExitStack
import concourse.bass as bass
import concourse.mybir as mybir
from concourse.bass_utils import run_bass_kernel_spmd

F32 = mybir.dt.float32
BF16 = mybir.dt.bfloat16
AF = mybir.ActivationFunctionType
ALU = mybir.AluOpType
AX = mybir.AxisListType

D = 2048
KT = 16
DFF = 8192
T = 512
CH = 128
NCH = T // CH
NDS = 40
SEM_LIMIT = 30000
EPS = 1e-6
NEGM = -30000.0

def _inproj_perm():
    o = {}
    c = 0
    names = ['u', 'rq', 'rk', 'rv', 'rg', 'wq', 'wk', 'wv', 'cq', 'ckv', 'kr']
    sizes = [512, 256, 256, 512, 512, 512, 128, 128, 384, 128, 64]
    for n, s in zip(names, sizes):
        o[n] = c
        c += s
    def swap64(base, nheads):
        idx = []
        for h in range(nheads):
            b = base + 64 * h
            idx += list(range(b + 32, b + 64)) + list(range(b, b + 32))
        return idx
    perm = []
    perm += list(range(o['u'], o['u'] + 512))
    perm += list(range(o['rq'], o['rq'] + 256))
    perm += swap64(o['rq'], 4)
    perm += list(range(o['rk'], o['rk'] + 256))
    perm += swap64(o['rk'], 4)
    perm += list(range(o['rg'], o['rg'] + 512))
    for i in range(4):
        perm += list(range(o['wq'] + 64 * i, o['wq'] + 64 * i + 64))
        perm += list(range(o['wq'] + 64 * (4 + i), o['wq'] + 64 * (4 + i) + 64))
    perm += list(range(o['wk'], o['wk'] + 128))
    perm += list(range(o['cq'], o['cq'] + 384))
    perm += list(range(o['ckv'], o['ckv'] + 128))
    perm += list(range(o['kr'], o['kr'] + 64)) * 2
    perm += swap64(o['kr'], 1) * 2
    perm += list(range(o['kr'], o['kr'] + 64)) * 2
    perm += list(range(o['rv'], o['rv'] + 512))
    perm += list(range(o['wv'], o['wv'] + 128))
    perm += list(range(o['wv'], o['wv'] + 128))
    return np.array(perm, dtype=np.int64)

INPERM = _inproj_perm()
NIN = len(INPERM)
NINB = NIN // 256


class Buf:
    __slots__ = ('name', 'w', 'r')

    def __init__(self, name):
        self.name = name
        self.w = None
        self.r = {}


class KB:
    def __init__(self, nc, es):
        self.nc = nc
        self.es = es
        self.E = {'pe': nc.tensor, 'dve': nc.vector, 'act': nc.scalar, 'pool': nc.gpsimd, 'sp': nc.sync}
        self.csem = {}
        self.cep = {e: 0 for e in ('pe', 'dve', 'act', 'pool')}
        self.ccnt = {e: 0 for e in self.cep}
        for e in self.cep:
            self.csem[(e, 0)] = es.enter_context(nc.semaphore('cs_%s_0' % e))
        self.dsem = [es.enter_context(nc.semaphore('ds%d' % i)) for i in range(NDS)]
        self.dcnt = [0] * NDS
        self.dnext = 0
        self.dnext2 = 0
        self.seen = {e: {} for e in self.E}
        self.nins = 0

    def _wait(self, eng, evs):
        need = {}
        for ev in evs:
            if ev is None:
                continue
            if ev[0] == 'c' and ev[1][0] == eng and eng == 'pe':
                continue
            key = (ev[0], ev[1])
            if need.get(key, 0) < ev[2]:
                need[key] = ev[2]
        for key, v in need.items():
            if self.seen[eng].get(key, 0) >= v:
                continue
            sem = self.csem[key[1]] if key[0] == 'c' else self.dsem[key[1]]
            self.E[eng].wait_ge(sem, v)
            self.seen[eng][key] = v

    @staticmethod
    def _deps(r, w):
        evs = []
        for b in r:
            evs.append(b.w)
        for b in w:
            evs.append(b.w)
            evs.extend(b.r.values())
        return evs

    @staticmethod
    def _commit(ev, r, w):
        for b in r:
            b.r[(ev[0], ev[1])] = ev
        for b in w:
            b.w = ev
            b.r = {}

    def op(self, eng, fn, r=(), w=()):
        self._wait(eng, self._deps(r, w))
        ins = fn(self.E[eng])
        if self.ccnt[eng] >= SEM_LIMIT:
            self.cep[eng] += 1
            self.ccnt[eng] = 0
            self.csem[(eng, self.cep[eng])] = self.es.enter_context(
                self.nc.semaphore('cs_%s_%d' % (eng, self.cep[eng])))
        self.ccnt[eng] += 1
        key = (eng, self.cep[eng])
        ins.then_inc(self.csem[key], 1)
        self._commit(('c', key, self.ccnt[eng]), r, w)
        self.nins += 1

    def dma(self, q, out, in_, r=(), w=(), **kw):
        half = NDS // 2
        if q == 'pool':
            si = half + self.dnext2
            self.dnext2 = (self.dnext2 + 1) % half
        else:
            si = self.dnext
            self.dnext = (self.dnext + 1) % half
        evs = self._deps(r, w)
        if self.dcnt[si] > 0:
            evs.append(('d', si, self.dcnt[si]))
        self._wait(q, evs)
        ins = self.E[q].dma_start(out=out, in_=in_, **kw)
        self.dcnt[si] += 16
        ins.then_inc(self.dsem[si], 16)
        ev = ('d', si, self.dcnt[si])
        self._commit(ev, r, w)
        self.nins += 1
        return ev

    def barrier(self):
        evs = [('c', (e, self.cep[e]), n) for e, n in self.ccnt.items() if n > 0]
        evs += [('d', i, n) for i, n in enumerate(self.dcnt) if n > 0]
        for e in self.E:
            self._wait(e, evs)


def build_program(S=4096, depth=2, mixers=True, dbg=None):
    NG = S // T
    nc = bass.Bass("TRN2", target_bir_lowering=False)
    es = ExitStack()
    kb = KB(nc, es)

    def din(name, shape, dt=F32):
        return nc.dram_tensor(name, list(shape), dt, kind="ExternalInput").ap()

    def dscr(name, shape, dt=BF16):
        return nc.dram_tensor(name, list(shape), dt).ap()

    x_d = din("x", [S, D])
    c_d = din("c", [16, 128])
    n1_d = din("norm1_g", [2, 16, 128])
    n2_d = din("norm2_g", [2, 16, 128])
    adaw_d = din("ada_w", [2, D, 6 * D])
    adab_d = din("ada_b", [2, 96, 128])
    win_d = din("w_in_ext", [2, D, NIN])
    wout_d = din("w_out", [2, D, D])
    w1_d = din("mlp_w1", [2, D, DFF])
    w2_d = din("mlp_w2", [2, DFF, D])
    fng_d = din("final_norm_g", [16, 128])
    ident_d = din("ident", [128, 128])
    out_d = nc.dram_tensor("out", [S, D], F32, kind="ExternalOutput").ap()
    rotC_d = din("rotC", [128, S])
    rotS_d = din("rotS", [128, S])
    swam_d = din("swa_mask", [2, 128, 256])
    mlam_d = din("mla_mask", [128, 128])
    decT_d = din("ret_decT", [128, 512])
    xi_d = din("ret_xi", [128, 2 * 512])
    zeta_d = din("ret_zeta", [128, 256])
    sink_d = din("swa_sinks", [2, 8])
    qn_d = din("mla_q_norm", [2, 3, 128])
    kvn_d = din("mla_kv_norm", [2, 1, 128])
    wuq_d = din("w_uq_ext", [2, 384, 1024])
    wuk_d = din("w_ukT", [2, 128, 512])
    wuv_d = din("w_uv", [2, 128, 512])
    glw_d = din("s5_glu_w", [2, 512, 512])
    lre_d = din("s5_lambda_re", [2, 32, 64])
    lim_d = din("s5_lambda_im", [2, 32, 64])
    ldt_d = din("s5_log_dt", [2, 1, 32])
    bre_d = din("s5_b_re", [2, 32, 64, 16])
    bim_d = din("s5_b_im", [2, 32, 64, 16])
    cre_d = din("s5_c_re", [2, 512, 64])
    cim_d = din("s5_c_im", [2, 512, 64])
    sd_d = din("s5_d", [2, 4, 128])
    s5c_d = din("s5_consts", [128, 640])
    glb_d = din("s5_glu_b", [2, 4, 128])
    dbgy_d = nc.dram_tensor("dbg_y", [NG, 128, KT * T], BF16, kind="ExternalOutput").ap() if dbg == 'ycat' else None

    win_s = [dscr("win_s%d" % l, [NINB, 128, KT * 256]) for l in range(2)]
    wout_s = [dscr("wout_s%d" % l, [8, 128, KT * 256]) for l in range(2)]
    w1_s = [dscr("w1_s%d" % l, [32, 128, KT * 256]) for l in range(2)]
    w2_s = [dscr("w2_s%d" % l, [32, 128, 8 * 512]) for l in range(2)]
    h_s = dscr("h_s", [NG, 128, KT * T], F32)

    def sb(name, shape, dt):
        return es.enter_context(nc.sbuf_tensor("sb_" + name, list(shape), dt))

    ps = es.enter_context(nc.psum_tensor("ps", [128, 8, 512], F32))
    bank = [Buf("bank%d" % i) for i in range(8)]
    rr = {'mm': 0, 'tr': 0}

    def mmbank():
        i = rr['mm']
        rr['mm'] = (i + 1) % 4
        return i

    def trbank():
        i = 4 + rr['tr']
        rr['tr'] = (rr['tr'] + 1) % 2
        return i

    ident = sb("ident", [128, 128], F32)
    identb = sb("identb", [128, 128], BF16)
    onesb = sb("onesb", [128, 128], BF16)
    cB = Buf("consts")
    hT = sb("hT", [128, KT, T], F32)
    hB = [Buf("h%d" % i) for i in range(KT)]
    aT = sb("aT", [128, KT, T], BF16)
    aB = [Buf("a%d" % i) for i in range(KT)]
    swork = sb("swork", [128, 2, 512], F32)
    sworkB = [Buf("sw0"), Buf("sw1")]
    rstd = swork[:, 0, 0:T]
    rstdB = sworkB[0]
    tmpf = sb("tmpf", [128, 2, T], F32)
    tmpB = [Buf("tmp0"), Buf("tmp1")]
    NW = 2
    wb = sb("wb", [128, NW, 4096], BF16)
    wB = [Buf("wb%d" % i) for i in range(NW)]
    hid = sb("hid", [128, 8, T], BF16)
    hidB = [Buf("hid")] * 8
    xin = hid[:].rearrange("p a t -> p (a t)").bitcast(F32).rearrange("p (o d) -> p o d", o=1)
    xinB = [hidB[0], hidB[0]]
    pv = sb("pv", [128, 2, 8, KT], F32)
    pvB = Buf("pv")
    epsc = sb("epsc", [128, 1], F32)

    kb.dma('sp', ident[:], ident_d[:, :], w=[cB])
    kb.dma('pool', identb[:], ident_d[:, :], w=[cB])
    kb.op('dve', lambda e: e.memset(onesb[:], 1.0), w=[cB])
    kb.op('dve', lambda e: e.memset(epsc[:], EPS), w=[cB])

    wsB = {}

    def cast(key, dst, src, kt, ncol):
        b = Buf("ws_%s" % (key,))
        wsB[key] = b
        kb.dma('pool', dst.rearrange("p (kt c) -> p kt c", c=ncol),
               src.rearrange("(kt p) c -> p kt c", p=128), w=[b])

    def cast_layer(l):
        for nb in range(NINB):
            cast(('in', l, nb), win_s[l][nb], win_d[l, :, nb * 256:(nb + 1) * 256], KT, 256)
        for nb in range(8):
            cast(('out', l, nb), wout_s[l][nb], wout_d[l, :, nb * 256:(nb + 1) * 256], KT, 256)
        for sl in range(8):
            for j in range(4):
                nb = sl * 4 + j
                cast(('w1', l, nb), w1_s[l][nb], w1_d[l, :, nb * 256:(nb + 1) * 256], KT, 256)
            for j in range(4):
                cast(('w2', l, sl * 4 + j), w2_s[l][sl * 4 + j],
                     w2_d[l, sl * 1024:(sl + 1) * 1024, j * 512:(j + 1) * 512], 8, 512)

    def load_fm(dst_ap, src_rows_ap, n, dstB):
        bi = 0 if rr.setdefault('xi', 0) == 0 else 1
        rr['xi'] = 1 - rr['xi']
        kb.dma('sp', xin[:n, 0, 0:128], src_rows_ap, w=[xinB[0]])
        b = trbank()
        kb.op('pe', lambda e: e.transpose(ps[:, b, 0:n], xin[:n, 0, 0:128], ident[:n, :n]),
              r=[xinB[0], cB], w=[bank[b]])
        kb.op('dve', lambda e: e.tensor_copy(out=dst_ap, in_=ps[:, b, 0:n]), r=[bank[b]], w=[dstB])

    with ExitStack() as es2:
        adaw = es2.enter_context(nc.sbuf_tensor("sb_adaw", [128, 2, KT * 384], BF16))
        adawB = [Buf("adaw0"), Buf("adaw1")]
        cact = es2.enter_context(nc.sbuf_tensor("sb_cact", [128, KT], F32))
        cactb = es2.enter_context(nc.sbuf_tensor("sb_cactb", [128, KT], BF16))
        cactB = Buf("cact")
        modT = es2.enter_context(nc.sbuf_tensor("sb_modT", [128, 2, 96], F32))
        modB = Buf("mod")
        abT = es2.enter_context(nc.sbuf_tensor("sb_abT", [128, 2, 96], F32))
        abB = Buf("ab")
        gT = es2.enter_context(nc.sbuf_tensor("sb_gT", [128, 5, KT], F32))
        gB = Buf("g")
        load_fm(cact[:], c_d[:, :], 16, cactB)
        kb.op('act', lambda e: e.activation(out=cactb[:], in_=cact[:], func=AF.Silu), r=[cactB], w=[cactB])
        for l in range(depth):
            load_fm(abT[:, l, :], adab_d[l], 96, abB)
            load_fm(gT[:, l, :], n1_d[l], 16, gB)
            load_fm(gT[:, 2 + l, :], n2_d[l], 16, gB)
        load_fm(gT[:, 4, :], fng_d[:, :], 16, gB)
        if dbg != 'io':
            cast_layer(0)
        ci = 0
        if dbg in ('io', 'cast'):
            kb.op('dve', lambda e: e.memset(pv[:], 1.0), w=[pvB])
        for l in range(depth if dbg not in ('io', 'cast') else 0):
            mb = mmbank()
            for cb in range(32):
                bi = ci % 2
                ci += 1
                kb.dma('pool', adaw[:, bi, :].rearrange("p (kt c) -> p kt c", c=384),
                       adaw_d[l, :, cb * 384:(cb + 1) * 384].rearrange("(kt p) c -> p kt c", p=128),
                       w=[adawB[bi]])
                for j in range(3):
                    col = cb * 3 + j
                    for kt in range(KT):
                        kb.op('pe', lambda e, kt=kt, j=j, col=col, bi=bi: e.matmul(
                            ps[:, mb, col:col + 1],
                            adaw[:, bi, kt * 384 + j * 128: kt * 384 + (j + 1) * 128],
                            cactb[:, kt:kt + 1], start=(kt == 0), stop=(kt == KT - 1)),
                            r=[adawB[bi], cactB], w=[bank[mb]])
            kb.op('dve', lambda e, l=l, mb=mb: e.tensor_tensor(out=modT[:, l, :], in0=ps[:, mb, 0:96],
                                                               in1=abT[:, l, :], op=ALU.add),
                  r=[bank[mb], abB], w=[modB])
            for (dst, gi, sc, sh, gt) in ((0, l, 16, 0, 32), (3, 2 + l, 64, 48, 80)):
                kb.op('dve', lambda e, dst=dst, gi=gi, sc=sc, l=l: e.scalar_tensor_tensor(
                    out=pv[:, l, dst, :], in0=modT[:, l, sc:sc + 16], scalar=1.0, in1=gT[:, gi, :],
                    op0=ALU.add, op1=ALU.mult), r=[modB, gB], w=[pvB])
                kb.op('dve', lambda e, dst=dst, sh=sh, l=l: e.tensor_copy(out=pv[:, l, dst + 1, :],
                                                                       in_=modT[:, l, sh:sh + 16]),
                      r=[modB], w=[pvB])
                kb.op('dve', lambda e, dst=dst, gt=gt, l=l: e.tensor_copy(out=pv[:, l, dst + 2, :],
                                                                       in_=modT[:, l, gt:gt + 16]),
                      r=[modB], w=[pvB])
            kb.op('dve', lambda e, l=l: e.tensor_copy(out=pv[:, l, 6, :], in_=gT[:, 4, :]), r=[gB], w=[pvB])
        if depth > 1 and dbg != 'io':
            cast_layer(1)
        kb.barrier()

    wstate = {'i': 0}

    def wload(key, src_ap, nbytes_cols):
        i = wstate['i'] % NW
        wstate['i'] += 1
        kb.dma('sp', wb[:, i, 0:nbytes_cols], src_ap, r=[wsB[key]], w=[wB[i]])
        return i

    def norm_mod(l, si):
        b = mmbank()
        for kt in range(KT):
            j = kt % 2
            kb.op('act', lambda e, kt=kt, j=j: e.activation(out=sq[:, j, :], in_=hT[:, kt, :], func=AF.Square),
                  r=[hB[kt]], w=[sqB[j]])
            kb.op('pe', lambda e, kt=kt, j=j: e.matmul(ps[:, b, 0:T], onesb[:], sq[:, j, :],
                                                      start=(kt == 0), stop=(kt == KT - 1)),
                  r=[sqB[j], cB], w=[bank[b]])
        kb.op('act', lambda e: e.activation(out=rstd, in_=ps[:, b, 0:T], func=AF.Sqrt,
                                            bias=epsc[:, 0:1], scale=1.0 / D),
              r=[bank[b], cB], w=[rstdB])
        kb.op('dve', lambda e: e.reciprocal(out=rstd, in_=rstd), r=[rstdB], w=[rstdB])
        for kt in range(KT):
            j = kt % 2
            kb.op('dve', lambda e, kt=kt, j=j: e.scalar_tensor_tensor(
                out=tmpf[:, j, :], in0=hT[:, kt, :], scalar=pv[:, l, si, kt:kt + 1], in1=rstd,
                op0=ALU.mult, op1=ALU.mult), r=[hB[kt], pvB, rstdB], w=[tmpB[j]])
            kb.op('act', lambda e, kt=kt, j=j: e.activation(
                out=aT[:, kt, :], in_=tmpf[:, j, :], func=AF.Identity, bias=pv[:, l, si + 1, kt:kt + 1],
                scale=1.0), r=[tmpB[j], pvB], w=[aB[kt]])

    def mlp(l):
        pend = []
        seq = []
        for sl in range(8):
            for j in range(4):
                seq.append(('w1', l, sl * 4 + j))
            for j in range(4):
                seq.append(('w2', l, sl * 4 + j))
        loaded = {}
        nxt = [0]

        def ensure(idx):
            while nxt[0] <= idx:
                n = nxt[0]
                key = seq[n]
                src = (w1_s if key[0] == 'w1' else w2_s)[l][key[2]]
                loaded[n] = wload(key, src, 4096)
                nxt[0] = n + 1
        pos = 0
        for sl in range(8):
            for j in range(4):
                ensure(min(pos + NW - 1, len(seq) - 1))
                wi = loaded[pos]
                pos += 1
                for jj in range(2):
                    ht = j * 2 + jj
                    b = mmbank()
                    for kt in range(KT):
                        kb.op('pe', lambda e, kt=kt, jj=jj, wi=wi, b=b: e.matmul(
                            ps[:, b, 0:T], wb[:, wi, kt * 256 + jj * 128: kt * 256 + (jj + 1) * 128],
                            aT[:, kt, :], start=(kt == 0), stop=(kt == KT - 1)),
                            r=[wB[wi], aB[kt]], w=[bank[b]])
                    tj = ht % 2
                    kb.op('act', lambda e, b=b, tj=tj: e.activation(out=tmpf[:, tj, :], in_=ps[:, b, 0:T], func=AF.Relu),
                          r=[bank[b]], w=[tmpB[tj]])
                    kb.op('dve', lambda e, ht=ht, tj=tj: e.tensor_tensor(out=hid[:, ht, :], in0=tmpf[:, tj, :],
                                                                         in1=tmpf[:, tj, :], op=ALU.mult),
                          r=[tmpB[tj]], w=[hidB[ht]])
            for j in range(4):
                ensure(min(pos + NW - 1, len(seq) - 1))
                wi = loaded[pos]
                pos += 1
                for jj in range(4):
                    ot = j * 4 + jj
                    b = mmbank()
                    for kt in range(8):
                        kb.op('pe', lambda e, kt=kt, jj=jj, wi=wi, b=b: e.matmul(
                            ps[:, b, 0:T], wb[:, wi, kt * 512 + jj * 128: kt * 512 + (jj + 1) * 128],
                            hid[:, kt, :], start=(kt == 0), stop=(kt == 7)),
                            r=[wB[wi], hidB[kt]], w=[bank[b]])
                    kb.op('dve', lambda e, ot=ot, b=b: e.scalar_tensor_tensor(
                        out=hT[:, ot, :], in0=ps[:, b, 0:T], scalar=pv[:, l, 5, ot:ot + 1], in1=hT[:, ot, :],
                        op0=ALU.mult, op1=ALU.add), r=[bank[b], pvB, hB[ot]], w=[hB[ot]])

    def load_x_group(g):
        for c in range(NCH):
            bi = c % 2
            kb.dma('pool', xin[:, 0, :], x_d[g * T + c * CH: g * T + (c + 1) * CH, :], w=[xinB[0]])
            for kt4 in range(4):
                b = trbank()
                for q in range(4):
                    kt = kt4 * 4 + q
                    kb.op('pe', lambda e, kt=kt, q=q, b=b, bi=bi: e.transpose(
                        ps[:, b, q * 128:(q + 1) * 128], xin[:, 0, kt * 128:(kt + 1) * 128], ident[:]),
                        r=[xinB[0], cB], w=[bank[b]])
                kb.op('act', lambda e, kt4=kt4, b=b, c=c: e.activation(
                    out=hT[:, kt4 * 4:(kt4 + 1) * 4, c * CH:(c + 1) * CH],
                    in_=ps[:, b, :].rearrange("p (q t) -> p q t", q=4), func=AF.Copy),
                    r=[bank[b]], w=[hB[kt4 * 4 + q] for q in range(4)])

    def final_out_group(g, l):
        b = mmbank()
        for kt in range(KT):
            j = kt % 2
            kb.op('act', lambda e, kt=kt, j=j: e.activation(out=sq[:, j, :], in_=hT[:, kt, :], func=AF.Square),
                  r=[hB[kt]], w=[sqB[j]])
            kb.op('pe', lambda e, kt=kt, j=j: e.matmul(ps[:, b, 0:T], onesb[:], sq[:, j, :],
                                                      start=(kt == 0), stop=(kt == KT - 1)),
                  r=[sqB[j], cB], w=[bank[b]])
        kb.op('act', lambda e: e.activation(out=rstd, in_=ps[:, b, 0:T], func=AF.Sqrt,
                                            bias=epsc[:, 0:1], scale=1.0 / D),
              r=[bank[b], cB], w=[rstdB])
        kb.op('dve', lambda e: e.reciprocal(out=rstd, in_=rstd), r=[rstdB], w=[rstdB])
        for kt in range(KT):
            kb.op('dve', lambda e, kt=kt: e.scalar_tensor_tensor(
                out=hT[:, kt, :], in0=hT[:, kt, :], scalar=pv[:, l, 6, kt:kt + 1], in1=rstd,
                op0=ALU.mult, op1=ALU.mult), r=[hB[kt], pvB, rstdB], w=[hB[kt]])
        for c in range(NCH):
            bi = c % 2
            for kt4 in range(4):
                b2 = trbank()
                for q in range(4):
                    kt = kt4 * 4 + q
                    kb.op('pe', lambda e, kt=kt, q=q, b2=b2, c=c: e.transpose(
                        ps[:, b2, q * 128:(q + 1) * 128], hT[:, kt, c * CH:(c + 1) * CH], ident[:]),
                        r=[hB[kt], cB], w=[bank[b2]])
                kb.op('act', lambda e, kt4=kt4, b2=b2, bi=bi: e.activation(
                    out=xin[:, 0, kt4 * 512:(kt4 + 1) * 512], in_=ps[:, b2, :], func=AF.Copy),
                    r=[bank[b2]], w=[xinB[0]])
            outev.append(kb.dma('pool', out_d[g * T + c * CH: g * T + (c + 1) * CH, :], xin[:, 0, :],
                                r=[xinB[0]]))


    NBLK = S // CH
    ycT = aT
    yB = aB
    rotC = sb("rotC", [128, T], F32)
    rotS = sb("rotS", [128, T], F32)
    rotB = Buf("rot")
    mk = sb("mk", [128, 2, 256], F32)
    mlam = sb("mlam", [128, 128], BF16)
    decT = sb("decT", [128, 512], F32)
    xit = sb("xit", [128, 2, 128], F32)
    zet = sb("zet", [128, 256], F32)
    sinkt = sb("sinkt", [128, 2, 8], F32)
    lnv = sb("lnv", [128, 2, 4], F32)
    glb = sb("glb", [128, 2, 4], F32)
    lB = Buf("layerconsts")
    uT = sb("uT", [128, 4, T], BF16)
    uB = Buf("uT")
    rq = sb("rq", [128, 2, T], BF16)
    rqx = sb("rqx", [128, 2, T], BF16)
    rk = sb("rk", [128, 2, T], BF16)
    rgs = sb("rgs", [128, 4, T], BF16)
    rvt = sb("rvt", [128, NCH, 512], BF16)
    retB = Buf("retproj")
    wq = sb("wq", [128, 4, T], BF16)
    wk = sb("wk", [128, CH + T], BF16)
    wvt = sb("wvt", [128, NCH + 1, 128], BF16)
    swaB = Buf("swaproj")
    cq = hid[:, 0:6, :].rearrange("p a t -> p (a t)").bitcast(F32).rearrange("p (j t) -> p j t", j=3)
    ckv = hid[:, 6:8, :].rearrange("p a t -> p (a t)").bitcast(F32)
    mlaB = Buf("mlaproj")
    ckT = sb("ckT", [128, S], BF16)
    krT = sb("krT", [128, S], BF16)
    cktok = sb("cktok", [128, NBLK, 128], BF16)
    cacheB = Buf("mlacache")
    wuq = sb("wuq", [128, 3, 1024], BF16)
    wuk = sb("wuk", [128, 512], BF16)
    wuv = sb("wuv", [128, 512], BF16)
    glw = sb("glw", [128, 4, 512], BF16)
    cqn = rvt[:, 0:3, :]
    qn = uT
    qa = rgs
    qr = sb("qr", [128, 2, T], BF16)
    mqB = Buf("mlaq")
    UT = wq
    UTB = swaB
    sm = sb("sm", [128, 64], F32)
    smB = Buf("sm")
    pbf = sb("pbf", [128, 2, 512], BF16)
    pbfB = [Buf("pb0"), Buf("pb1")]
    sq, sqB = pbf, pbfB
    ptb = sb("ptb", [128, 2, 512], BF16)
    ptbB = [Buf("pt0"), Buf("pt1")]
    osb = sb("osb", [128, 512], BF16)
    osbB = Buf("osb")
    Sst = sb("Sst", [128, 2, 128], F32)
    Sbf = sb("Sbf", [128, 2, 128], BF16)
    SB_ = Buf("retstate")
    rr['at'] = 0

    def atbank():
        i = 6 + rr['at']
        rr['at'] = 1 - rr['at']
        return i

    def psbf(b):
        return ps[:, b, :].bitcast(BF16)

    kb.dma('sp', mk[:], swam_d.rearrange("a p c -> p a c"), w=[cB])
    kb.dma('pool', mlam[:], mlam_d[:, :], w=[cB])
    kb.dma('sp', decT[:], decT_d[:, :], w=[cB])
    kb.dma('sp', xit[:], xi_d.rearrange("p (a c) -> p a c", a=2)[:, :, 0:128], w=[cB])
    kb.dma('sp', zet[:], zeta_d[:, :], w=[cB])
    for l in range(2):
        kb.dma('sp', sinkt[:, l, :], sink_d[l:l + 1, :].partition_broadcast(128), w=[cB])

    def layer_setup(l):
        for j in range(3):
            load_fm(lnv[:, l, j:j + 1], qn_d[l, j:j + 1, :], 1, lB)
        load_fm(lnv[:, l, 3:4], kvn_d[l, 0:1, :], 1, lB)
        load_fm(glb[:, l, :], glb_d[l], 4, lB)
        kb.dma('pool', wuq[:], wuq_d[l].rearrange("(kt p) c -> p kt c", p=128), w=[lB])
        kb.dma('pool', wuk[:], wuk_d[l], w=[lB])
        kb.dma('pool', wuv[:], wuv_d[l], w=[lB])
        kb.dma('pool', glw[:], glw_d[l].rearrange("(kt p) c -> p kt c", p=128), w=[lB])
        kb.op('dve', lambda e: e.memset(wk[:, 0:CH], 0.0), w=[swaB])
        kb.op('dve', lambda e: e.memset(wvt[:, 0, :], 0.0), w=[swaB])
        kb.op('dve', lambda e: e.memset(Sst[:], 0.0), w=[SB_])
        kb.op('dve', lambda e: e.memset(Sbf[:], 0.0), w=[SB_])

    def in_proj(l, g):
        kb.dma('sp', rotC[:], rotC_d[:, g * T:(g + 1) * T], w=[rotB])
        kb.dma('sp', rotS[:], rotS_d[:, g * T:(g + 1) * T], w=[rotB])
        if g > 0:
            kb.op('dve', lambda e: e.tensor_copy(out=wk[:, 0:CH], in_=wk[:, T:T + CH]), r=[swaB], w=[swaB])
            kb.op('dve', lambda e: e.tensor_copy(out=wvt[:, 0, :], in_=wvt[:, NCH, :]), r=[swaB], w=[swaB])
        loaded = {}
        nxt = [0]

        def ensure(idx):
            while nxt[0] <= min(idx, NINB - 1):
                n = nxt[0]
                loaded[n] = wload(('in', l, n), win_s[l][n], 4096)
                nxt[0] = n + 1

        def evac(ti, b):
            P = ps[:, b, 0:T]
            if ti < 4:
                kb.op('act', lambda e: e.activation(out=uT[:, ti, :], in_=P, func=AF.Copy), r=[bank[b]], w=[uB])
            elif ti in (4, 5, 8, 9):
                j = ti % 2
                kb.op('dve', lambda e: e.tensor_tensor(out=tmpf[:, j, :], in0=P, in1=rotC[:], op=ALU.mult),
                      r=[bank[b], rotB], w=[*tmpB])
            elif ti in (6, 7, 10, 11):
                j = ti % 2
                kb.op('dve', lambda e: e.tensor_tensor(out=swork[:, 0, 0:T], in0=P, in1=rotS[:], op=ALU.mult),
                      r=[bank[b], rotB], w=[sworkB[0]])
                if ti < 8:
                    kb.op('dve', lambda e: e.tensor_tensor(out=tmpf[:, j, :], in0=tmpf[:, j, :], in1=swork[:, 0, 0:T],
                                                          op=ALU.add), r=[*tmpB, sworkB[0]], w=[*tmpB])
                    kb.op('act', lambda e: e.activation(out=rq[:, j, :], in_=tmpf[:, j, :], func=AF.Copy),
                          r=[*tmpB], w=[retB])
                    kb.op('dve', lambda e: e.tensor_tensor(out=rqx[:, j, :].rearrange("p (c t) -> p c t", c=NCH),
                                                          in0=tmpf[:, j, :].rearrange("p (c t) -> p c t", c=NCH),
                                                          in1=xit[:, j, :].unsqueeze(1).to_broadcast([128, NCH, 128]),
                                                          op=ALU.mult), r=[*tmpB, cB], w=[retB])
                else:
                    kb.op('dve', lambda e: e.tensor_tensor(out=rk[:, j, :], in0=tmpf[:, j, :], in1=swork[:, 0, 0:T],
                                                          op=ALU.add), r=[*tmpB, sworkB[0]], w=[retB])
            elif ti < 16:
                kb.op('act', lambda e: e.activation(out=rgs[:, ti - 12, :], in_=P, func=AF.Silu),
                      r=[bank[b]], w=[retB])
            elif ti < 20:
                kb.op('act', lambda e: e.activation(out=wq[:, ti - 16, :], in_=P, func=AF.Copy),
                      r=[bank[b]], w=[swaB])
            elif ti == 20:
                kb.op('act', lambda e: e.activation(out=wk[:, CH:CH + T], in_=P, func=AF.Copy),
                      r=[bank[b]], w=[swaB])
            elif ti < 24:
                kb.op('act', lambda e: e.activation(out=cq[:, ti - 21, :], in_=P, func=AF.Copy),
                      r=[bank[b]], w=[mlaB] + hidB)
            elif ti == 24:
                kb.op('act', lambda e: e.activation(out=ckv[:], in_=P, func=AF.Copy), r=[bank[b]], w=[mlaB] + hidB)
            elif ti == 25:
                kb.op('dve', lambda e: e.tensor_tensor(out=swork[:, 1, 0:T], in0=P, in1=rotC[:], op=ALU.mult),
                      r=[bank[b], rotB], w=[sworkB[1]])
            elif ti == 26:
                kb.op('dve', lambda e: e.tensor_tensor(out=swork[:, 0, 0:T], in0=P, in1=rotS[:], op=ALU.mult),
                      r=[bank[b], rotB], w=[sworkB[0]])
                kb.op('dve', lambda e: e.tensor_tensor(out=krT[:, g * T:(g + 1) * T], in0=swork[:, 0, 0:T],
                                                      in1=swork[:, 1, 0:T], op=ALU.add),
                      r=sworkB, w=[cacheB])

        for nb in range(NINB):
            ensure(nb + NW - 1)
            wi = loaded[nb]
            if nb < 14:
                for jj in range(2):
                    ti = nb * 2 + jj
                    if ti == 27:
                        continue
                    b = mmbank()
                    for kt in range(KT):
                        kb.op('pe', lambda e, kt=kt, jj=jj, wi=wi, b=b: e.matmul(
                            ps[:, b, 0:T], wb[:, wi, kt * 256 + jj * 128: kt * 256 + (jj + 1) * 128],
                            aT[:, kt, :], start=(kt == 0), stop=(kt == KT - 1)),
                            r=[wB[wi], aB[kt]], w=[bank[b]])
                    evac(ti, b)
            else:
                for c in range(NCH):
                    b = mmbank()
                    ncol = 256 if nb < 16 else 128
                    for kt in range(KT):
                        kb.op('pe', lambda e, kt=kt, wi=wi, b=b, c=c, ncol=ncol: e.matmul(
                            ps[:, b, 0:ncol], aT[:, kt, c * CH:(c + 1) * CH],
                            wb[:, wi, kt * 256: kt * 256 + ncol], start=(kt == 0), stop=(kt == KT - 1)),
                            r=[wB[wi], aB[kt]], w=[bank[b]])
                    if nb < 16:
                        kb.op('act', lambda e, b=b, c=c, nb=nb: e.activation(
                            out=rvt[:, c, (nb - 14) * 256:(nb - 13) * 256], in_=ps[:, b, 0:256], func=AF.Copy),
                            r=[bank[b]], w=[retB])
                    else:
                        kb.op('act', lambda e, b=b, c=c: e.activation(
                            out=wvt[:, 1 + c, :], in_=ps[:, b, 0:128], func=AF.Copy), r=[bank[b]], w=[swaB])

    def out_proj(l):
        loaded = {}
        nxt = [0]

        def ensure(idx):
            while nxt[0] <= min(idx, 7):
                n = nxt[0]
                loaded[n] = wload(('out', l, n), wout_s[l][n], 4096)
                nxt[0] = n + 1
        for nb in range(8):
            ensure(nb + NW - 1)
            wi = loaded[nb]
            for jj in range(2):
                ot = nb * 2 + jj
                b = mmbank()
                for kt in range(KT):
                    kb.op('pe', lambda e, kt=kt, jj=jj, wi=wi, b=b: e.matmul(
                        ps[:, b, 0:T], wb[:, wi, kt * 256 + jj * 128: kt * 256 + (jj + 1) * 128],
                        ycT[:, kt, :], start=(kt == 0), stop=(kt == KT - 1)),
                        r=[wB[wi], yB[kt]], w=[bank[b]])
                kb.op('dve', lambda e, ot=ot, b=b: e.scalar_tensor_tensor(
                    out=hT[:, ot, :], in0=ps[:, b, 0:T], scalar=pv[:, l, 2, ot:ot + 1], in1=hT[:, ot, :],
                    op0=ALU.mult, op1=ALU.add), r=[bank[b], pvB, hB[ot]], w=[hB[ot]])

    def swa(l, g):
        for c in range(NCH):
            n = g * NCH + c
            mi = 1 if n == 0 else 0
            ob = atbank()
            for i in range(4):
                sj = i % 2
                for hf in range(2):
                    P0 = 64 * hf
                    b = mmbank()
                    kb.op('pe', lambda e, i=i, hf=hf, P0=P0, b=b, c=c: e.matmul(
                        ps[:, b, 0:256], wq[P0:P0 + 64, i, c * CH:(c + 1) * CH],
                        wk[P0:P0 + 64, c * CH: c * CH + 256], start=True, stop=True),
                        r=[swaB], w=[bank[b]])
                    kb.op('dve', lambda e, b=b, sj=sj, mi=mi, hf=hf: e.tensor_tensor(
                        out=swork[:, sj, hf * 256:(hf + 1) * 256], in0=ps[:, b, 0:256],
                        in1=mk[:, mi, :], op=ALU.add),
                        r=[bank[b], cB], w=[sworkB[sj]])
                kb.op('dve', lambda e, sj=sj: e.tensor_reduce(
                    out=sm[:, 0:2], in_=swork[:, sj, :].rearrange("p (a c) -> p a c", a=2), axis=AX.X, op=ALU.max),
                    r=[sworkB[sj]], w=[smB])
                kb.op('dve', lambda e: e.tensor_scalar(out=sm[:, 0:2], in0=sm[:, 0:2], scalar1=0.125, scalar2=None,
                                                      op0=ALU.mult), r=[smB], w=[smB])
                for hf in range(2):
                    hd = i + 4 * hf
                    kb.op('dve', lambda e, hf=hf, hd=hd: e.tensor_tensor(
                        out=sm[:, hf:hf + 1], in0=sm[:, hf:hf + 1], in1=sinkt[:, l, hd:hd + 1], op=ALU.max),
                        r=[smB, cB], w=[smB])
                kb.op('dve', lambda e: e.tensor_scalar(out=sm[:, 2:4], in0=sm[:, 0:2], scalar1=-1.0, scalar2=None,
                                                      op0=ALU.mult), r=[smB], w=[smB])
                for hf in range(2):
                    hd = i + 4 * hf
                    kb.op('act', lambda e, hf=hf, sj=sj: e.activation(
                        out=pbf[:, sj, hf * 256:(hf + 1) * 256], in_=swork[:, sj, hf * 256:(hf + 1) * 256],
                        func=AF.Exp, bias=sm[:, 2 + hf:3 + hf], scale=0.125, accum_out=sm[:, 4 + hf:5 + hf]),
                        r=[sworkB[sj], smB], w=[pbfB[sj], smB])
                    kb.op('act', lambda e, hf=hf, hd=hd: e.activation(
                        out=sm[:, 6 + hf:7 + hf], in_=sinkt[:, l, hd:hd + 1], func=AF.Exp,
                        bias=sm[:, 2 + hf:3 + hf], scale=1.0), r=[smB, cB], w=[smB])
                kb.op('dve', lambda e: e.tensor_tensor(out=sm[:, 4:6], in0=sm[:, 4:6], in1=sm[:, 6:8], op=ALU.add),
                      r=[smB], w=[smB])
                for hf in range(2):
                    hd = i + 4 * hf
                    kb.op('dve', lambda e, hf=hf, hd=hd: e.reciprocal(out=sm[:, 8 + hd:9 + hd], in_=sm[:, 4 + hf:5 + hf]),
                          r=[smB], w=[smB])
                tb = trbank()
                for q4 in range(4):
                    kb.op('pe', lambda e, q4=q4, tb=tb, sj=sj: e.transpose(
                        psbf(tb)[:, q4 * 128:(q4 + 1) * 128], pbf[:, sj, q4 * 128:(q4 + 1) * 128], identb[:]),
                        r=[pbfB[sj], cB], w=[bank[tb]])
                kb.op('act', lambda e, tb=tb, sj=sj: e.activation(out=ptb[:, sj, :], in_=psbf(tb)[:, 0:512], func=AF.Copy),
                      r=[bank[tb]], w=[ptbB[sj]])
                for hf in range(2):
                    hd = i + 4 * hf
                    for blk in range(2):
                        kb.op('pe', lambda e, hf=hf, hd=hd, blk=blk, sj=sj, c=c, ob=ob: e.matmul(
                            ps[:, ob, hd * 64:(hd + 1) * 64], ptb[:, sj, (hf * 2 + blk) * 128:(hf * 2 + blk + 1) * 128],
                            wvt[:, c + blk, hf * 64:(hf + 1) * 64], start=(blk == 0), stop=(blk == 1)),
                            r=[ptbB[sj], swaB], w=[bank[ob]])
            kb.op('dve', lambda e, ob=ob: e.tensor_tensor(
                out=osb[:].rearrange("p (h d) -> p h d", h=8), in0=ps[:, ob, :].rearrange("p (h d) -> p h d", h=8),
                in1=sm[:, 8:16].unsqueeze(2).to_broadcast([128, 8, 64]), op=ALU.mult),
                r=[bank[ob], smB], w=[osbB])
            tb = trbank()
            for j in range(4):
                kb.op('pe', lambda e, j=j, tb=tb: e.transpose(
                    psbf(tb)[:, j * 128:(j + 1) * 128], osb[:, j * 128:(j + 1) * 128], identb[:]),
                    r=[osbB, cB], w=[bank[tb]])
            kb.op('act', lambda e, tb=tb, c=c: e.activation(
                out=ycT[:, 8:12, c * CH:(c + 1) * CH], in_=psbf(tb)[:, 0:512].rearrange("p (j t) -> p j t", j=4),
                func=AF.Copy), r=[bank[tb]], w=yB[8:12])

    GAM = [float(np.exp(np.float64(128.0) * np.log1p(-(2.0 ** (-5.0 - h))))) for h in range(4)]

    def ret(l, g):
        for c in range(NCH):
            cs = slice(c * CH, (c + 1) * CH)
            bb = [mmbank(), mmbank()]
            for hd in range(4):
                pr, hh = hd // 2, hd % 2
                P0 = 64 * hh
                b = bb[hh]
                kb.op('pe', lambda e, hd=hd, pr=pr, P0=P0, b=b, cs=cs: e.matmul(
                    ps[:, b, hd * 128:(hd + 1) * 128], rk[P0:P0 + 64, pr, cs], rq[P0:P0 + 64, pr, cs],
                    start=True, stop=True), r=[retB], w=[bank[b]])
            for hd in range(4):
                b = bb[hd % 2]
                kb.op('dve', lambda e, b=b, hd=hd: e.tensor_tensor(
                    out=pbf[:, 0, hd * 128:(hd + 1) * 128], in0=ps[:, b, hd * 128:(hd + 1) * 128],
                    in1=decT[:, hd * 128:(hd + 1) * 128], op=ALU.mult),
                    r=[bank[b], cB], w=[pbfB[0]])
            obs = [6, 7]
            for hd in range(4):
                pr, hh = hd // 2, hd % 2
                P0 = 64 * hh
                ob = obs[hh]
                kb.op('pe', lambda e, hd=hd, ob=ob, c=c: e.matmul(
                    ps[:, ob, hd * 128:(hd + 1) * 128], pbf[:, 0, hd * 128:(hd + 1) * 128],
                    rvt[:, c, hd * 128:(hd + 1) * 128], start=True, stop=False),
                    r=[pbfB[0], retB], w=[bank[ob]])
                kb.op('pe', lambda e, hd=hd, pr=pr, P0=P0, ob=ob, cs=cs: e.matmul(
                    ps[:, ob, hd * 128:(hd + 1) * 128], rqx[P0:P0 + 64, pr, cs], Sbf[P0:P0 + 64, pr, :],
                    start=False, stop=True), r=[retB, SB_], w=[bank[ob]])
            for hd in range(4):
                ob = obs[hd % 2]
                kb.op('act', lambda e, hd=hd, ob=ob: e.activation(
                    out=swork[:, 1, hd * 128:(hd + 1) * 128], in_=ps[:, ob, hd * 128:(hd + 1) * 128],
                    func=AF.Square, accum_out=sm[:, 16 + hd:17 + hd]), r=[bank[ob]], w=[sworkB[1], smB])
            kb.op('act', lambda e: e.activation(out=sm[:, 20:24], in_=sm[:, 16:20], func=AF.Sqrt,
                                                bias=epsc[:, 0:1], scale=1.0 / 128), r=[smB, cB], w=[smB])
            kb.op('dve', lambda e: e.reciprocal(out=sm[:, 20:24], in_=sm[:, 20:24]), r=[smB], w=[smB])
            for hd in range(4):
                ob = obs[hd % 2]
                kb.op('dve', lambda e, ob=ob, hd=hd: e.tensor_scalar(
                    out=osb[:, hd * 128:(hd + 1) * 128], in0=ps[:, ob, hd * 128:(hd + 1) * 128],
                    scalar1=sm[:, 20 + hd:21 + hd], scalar2=None, op0=ALU.mult),
                    r=[bank[ob], smB], w=[osbB])
            tb = trbank()
            for j in range(4):
                kb.op('pe', lambda e, j=j, tb=tb: e.transpose(
                    psbf(tb)[:, j * 128:(j + 1) * 128], osb[:, j * 128:(j + 1) * 128], identb[:]),
                    r=[osbB, cB], w=[bank[tb]])
            kb.op('dve', lambda e, tb=tb, cs=cs: e.tensor_tensor(
                out=ycT[:, 4:8, cs], in0=psbf(tb)[:, 0:512].rearrange("p (j t) -> p j t", j=4),
                in1=rgs[:, :, cs], op=ALU.mult), r=[bank[tb], retB], w=yB[4:8])
            tb = trbank()
            for pr in range(2):
                kb.op('pe', lambda e, pr=pr, tb=tb, cs=cs: e.transpose(
                    psbf(tb)[:, pr * 128:(pr + 1) * 128], rk[:, pr, cs], identb[:]),
                    r=[retB, cB], w=[bank[tb]])
            kb.op('dve', lambda e, tb=tb: e.tensor_tensor(out=pbf[:, 1, 0:256], in0=psbf(tb)[:, 0:256], in1=zet[:],
                                                         op=ALU.mult), r=[bank[tb], cB], w=[pbfB[1]])
            b2 = mmbank()
            for hd in range(4):
                pr = hd // 2
                kb.op('pe', lambda e, hd=hd, pr=pr, b2=b2, c=c: e.matmul(
                    ps[:, b2, hd * 128:(hd + 1) * 128], pbf[:, 1, pr * 128:(pr + 1) * 128],
                    rvt[:, c, hd * 128:(hd + 1) * 128], start=True, stop=True),
                    r=[pbfB[1], retB], w=[bank[b2]])
            for hd in range(4):
                pr, hh = hd // 2, hd % 2
                P0 = 64 * hh
                kb.op('dve', lambda e, hd=hd, pr=pr, P0=P0, b2=b2: e.scalar_tensor_tensor(
                    out=Sst[P0:P0 + 64, pr, :], in0=Sst[P0:P0 + 64, pr, :], scalar=GAM[hd],
                    in1=ps[P0:P0 + 64, b2, hd * 128:(hd + 1) * 128], op0=ALU.mult, op1=ALU.add),
                    r=[bank[b2], SB_], w=[SB_])
            kb.op('act', lambda e: e.activation(out=Sbf[:], in_=Sst[:], func=AF.Copy), r=[SB_], w=[SB_])

    MSC = float((128 + 64) ** -0.5)

    def mla(l, g):
        b = mmbank()
        for j in range(3):
            kb.op('act', lambda e, j=j: e.activation(out=sq[:, j % 2, :], in_=cq[:, j, :], func=AF.Square),
                  r=[mlaB] + hidB, w=[sqB[j % 2]])
            kb.op('pe', lambda e, j=j, b=b: e.matmul(ps[:, b, 0:T], onesb[:], sq[:, j % 2, :], start=(j == 0), stop=(j == 2)),
                  r=[sqB[j % 2], cB], w=[bank[b]])
        kb.op('act', lambda e, b=b: e.activation(out=swork[:, 0, 0:T], in_=ps[:, b, 0:T], func=AF.Sqrt,
                                                 bias=epsc[:, 0:1], scale=1.0 / 384), r=[bank[b], cB], w=[sworkB[0]])
        kb.op('dve', lambda e: e.reciprocal(out=swork[:, 0, 0:T], in_=swork[:, 0, 0:T]), r=[sworkB[0]], w=[sworkB[0]])
        for j in range(3):
            kb.op('dve', lambda e, j=j: e.scalar_tensor_tensor(
                out=cqn[:, j, :], in0=cq[:, j, :], scalar=lnv[:, l, j:j + 1], in1=swork[:, 0, 0:T],
                op0=ALU.mult, op1=ALU.mult), r=[mlaB, lB, sworkB[0], retB] + hidB, w=[mqB, retB])
        b = mmbank()
        kb.op('act', lambda e: e.activation(out=sq[:, 0, :], in_=ckv[:], func=AF.Square), r=[mlaB] + hidB, w=[sqB[0]])
        kb.op('pe', lambda e, b=b: e.matmul(ps[:, b, 0:T], onesb[:], sq[:, 0, :], start=True, stop=True),
              r=[sqB[0], cB], w=[bank[b]])
        kb.op('act', lambda e, b=b: e.activation(out=swork[:, 1, 0:T], in_=ps[:, b, 0:T], func=AF.Sqrt,
                                                 bias=epsc[:, 0:1], scale=1.0 / 128), r=[bank[b], cB], w=[sworkB[1]])
        kb.op('dve', lambda e: e.reciprocal(out=swork[:, 1, 0:T], in_=swork[:, 1, 0:T]), r=[sworkB[1]], w=[sworkB[1]])
        kb.op('dve', lambda e: e.scalar_tensor_tensor(
            out=ckT[:, g * T:(g + 1) * T], in0=ckv[:], scalar=lnv[:, l, 3:4], in1=swork[:, 1, 0:T],
            op0=ALU.mult, op1=ALU.mult), r=[mlaB, lB, sworkB[1]] + hidB, w=[cacheB])
        tb = trbank()
        for c in range(NCH):
            kb.op('pe', lambda e, c=c, tb=tb: e.transpose(
                psbf(tb)[:, c * 128:(c + 1) * 128], ckT[:, g * T + c * CH: g * T + (c + 1) * CH], identb[:]),
                r=[cacheB, cB], w=[bank[tb]])
        kb.op('act', lambda e, tb=tb: e.activation(
            out=cktok[:, g * NCH:(g + 1) * NCH, :], in_=psbf(tb)[:, 0:T].rearrange("p (c t) -> p c t", c=NCH),
            func=AF.Copy), r=[bank[tb]], w=[cacheB])
        for ot in range(8):
            b = mmbank()
            for kt in range(3):
                kb.op('pe', lambda e, kt=kt, ot=ot, b=b: e.matmul(
                    ps[:, b, 0:T], wuq[:, kt, ot * 128:(ot + 1) * 128], cqn[:, kt, :], start=(kt == 0), stop=(kt == 2)),
                    r=[lB, mqB], w=[bank[b]])
            if ot < 4:
                kb.op('act', lambda e, ot=ot, b=b: e.activation(out=qn[:, ot, :], in_=ps[:, b, 0:T], func=AF.Copy),
                      r=[bank[b]], w=[mqB])
            elif ot < 6:
                kb.op('dve', lambda e, ot=ot, b=b: e.tensor_tensor(out=tmpf[:, ot - 4, :], in0=ps[:, b, 0:T], in1=rotC[:],
                                                                   op=ALU.mult), r=[bank[b], rotB], w=[*tmpB])
            else:
                kb.op('dve', lambda e, b=b: e.tensor_tensor(out=swork[:, 0, 0:T], in0=ps[:, b, 0:T], in1=rotS[:], op=ALU.mult),
                      r=[bank[b], rotB], w=[sworkB[0]])
                kb.op('dve', lambda e, ot=ot: e.tensor_tensor(out=qr[:, ot - 6, :], in0=tmpf[:, ot - 6, :],
                                                              in1=swork[:, 0, 0:T], op=ALU.add),
                      r=[*tmpB, sworkB[0]], w=[mqB])
        for h in range(4):
            b = mmbank()
            kb.op('pe', lambda e, h=h, b=b: e.matmul(ps[:, b, 0:T], wuk[:, h * 128:(h + 1) * 128], qn[:, h, :],
                                                     start=True, stop=True), r=[lB, mqB], w=[bank[b]])
            kb.op('act', lambda e, h=h, b=b: e.activation(out=qa[:, h, :], in_=ps[:, b, 0:T], func=AF.Copy),
                  r=[bank[b]], w=[mqB])
        for c in range(NCH):
            n = g * NCH + c
            nb_ = n + 1
            nsb = (nb_ + 3) // 4
            cs = slice(c * CH, (c + 1) * CH)
            for h in range(4):
                P0 = 64 * (h % 2)

                def scores(b, sbk, w):
                    k0 = sbk * 512
                    last = (sbk == nsb - 1)
                    kb.op('pe', lambda e: e.matmul(ps[:, b, 0:w], qa[:, h, cs], ckT[:, k0:k0 + w], start=True, stop=False),
                          r=[mqB, cacheB], w=[bank[b]])
                    kb.op('pe', lambda e: e.matmul(ps[:, b, 0:w], qr[P0:P0 + 64, h // 2, cs], krT[P0:P0 + 64, k0:k0 + w],
                                                   start=False, stop=(not last)), r=[mqB, cacheB], w=[bank[b]])
                    if last:
                        kb.op('pe', lambda e: e.matmul(ps[:, b, w - 128:w], identb[:], mlam[:], start=False, stop=True),
                              r=[cB], w=[bank[b]])
                for sbk in range(nsb):
                    w = min(512, nb_ * 128 - sbk * 512)
                    b = atbank()
                    scores(b, sbk, w)
                    kb.op('dve', lambda e, b=b, sbk=sbk, w=w: e.tensor_reduce(
                        out=sm[:, 32 + sbk:33 + sbk], in_=ps[:, b, 0:w], axis=AX.X, op=ALU.max),
                        r=[bank[b]], w=[smB])
                kb.op('dve', lambda e: e.tensor_reduce(out=sm[:, 24:25], in_=sm[:, 32:32 + nsb], axis=AX.X, op=ALU.max),
                      r=[smB], w=[smB])
                kb.op('dve', lambda e: e.tensor_scalar(out=sm[:, 25:26], in0=sm[:, 24:25], scalar1=-MSC, scalar2=None,
                                                      op0=ALU.mult), r=[smB], w=[smB])
                ub = mmbank()
                for sbk in range(nsb):
                    w = min(512, nb_ * 128 - sbk * 512)
                    nk = w // 128
                    b = atbank()
                    scores(b, sbk, w)
                    pj = sbk % 2
                    kb.op('act', lambda e, b=b, sbk=sbk, w=w, pj=pj: e.activation(
                        out=pbf[:, pj, 0:w], in_=ps[:, b, 0:w], func=AF.Exp, bias=sm[:, 25:26], scale=MSC,
                        accum_out=sm[:, 40 + sbk:41 + sbk]), r=[bank[b], smB], w=[pbfB[pj], smB])
                    tb = trbank()
                    for q4 in range(nk):
                        kb.op('pe', lambda e, q4=q4, tb=tb, pj=pj: e.transpose(
                            psbf(tb)[:, q4 * 128:(q4 + 1) * 128], pbf[:, pj, q4 * 128:(q4 + 1) * 128], identb[:]),
                            r=[pbfB[pj], cB], w=[bank[tb]])
                    kb.op('dve', lambda e, tb=tb, pj=pj, w=w: e.tensor_copy(out=ptb[:, pj, 0:w], in_=psbf(tb)[:, 0:w]),
                          r=[bank[tb]], w=[ptbB[pj]])
                    for q4 in range(nk):
                        kblk = sbk * 4 + q4
                        kb.op('pe', lambda e, q4=q4, kblk=kblk, pj=pj, ub=ub: e.matmul(
                            ps[:, ub, 0:128], ptb[:, pj, q4 * 128:(q4 + 1) * 128], cktok[:, kblk, :],
                            start=(kblk == 0), stop=(kblk == nb_ - 1)), r=[ptbB[pj], cacheB], w=[bank[ub]])
                kb.op('dve', lambda e: e.tensor_reduce(out=sm[:, 26:27], in_=sm[:, 40:40 + nsb], axis=AX.X, op=ALU.add),
                      r=[smB], w=[smB])
                kb.op('dve', lambda e: e.reciprocal(out=sm[:, 27:28], in_=sm[:, 26:27]), r=[smB], w=[smB])
                kb.op('act', lambda e, ub=ub: e.activation(out=osb[:, 0:128], in_=ps[:, ub, 0:128], func=AF.Copy,
                                                         scale=sm[:, 27:28]), r=[bank[ub], smB], w=[osbB])
                tb = trbank()
                kb.op('pe', lambda e, tb=tb: e.transpose(psbf(tb)[:, 0:128], osb[:, 0:128], identb[:]),
                      r=[osbB, cB], w=[bank[tb]])
                kb.op('dve', lambda e, tb=tb, h=h, cs=cs: e.tensor_copy(out=UT[:, h, cs], in_=psbf(tb)[:, 0:128]),
                      r=[bank[tb]], w=[UTB])
        for h in range(4):
            b = mmbank()
            kb.op('pe', lambda e, h=h, b=b: e.matmul(ps[:, b, 0:T], wuv[:, h * 128:(h + 1) * 128], UT[:, h, :],
                                                     start=True, stop=True), r=[lB, UTB], w=[bank[b]])
            kb.op('act', lambda e, h=h, b=b: e.activation(out=ycT[:, 12 + h, :], in_=ps[:, b, 0:T], func=AF.Copy),
                  r=[bank[b]], w=[yB[12 + h]])


    TWO_PI = float(2 * np.pi)
    MAGIC = 12582912.0
    s5c = sb("s5c", [128, 640], F32)
    kb.dma('sp', s5c[:], s5c_d[:, :], w=[cB])
    trib = sb("trib", [128, 128], BF16)
    kb.dma('pool', trib[:], s5c_d[:, 0:128], w=[cB])
    Jm = s5c[:, 128:256]
    bmask = s5c[:, 256:264]
    mcol = s5c[:, 264:265]
    negmcol = s5c[:, 265:266]
    sgn1 = s5c[:, 266:267]
    onesrow = s5c[0:1, 384:512]
    trow = s5c[:, 512:640]
    Bblk = sb("Bblk", [128, 4, 1024], BF16)
    crT = sb("crT", [128, 32, 64], BF16)
    ciT = sb("ciT", [128, 32, 64], BF16)
    AT1 = sb("AT1", [128, 32, 128], BF16)
    AT2 = sb("AT2", [128, 32, 128], BF16)
    C1 = sb("C1", [128, 512], BF16)
    C2 = sb("C2", [128, 512], BF16)
    Dd = sb("Dd", [128, 4, 128], BF16)
    a128 = sb("a128", [128, 2, 32], F32)
    s5B = Buf("s5tables")
    Scol = sb("Scol", [128, 32], F32)
    SrB = Buf("Srow")
    zT = wq
    zB = swaB
    Qa, QaB, Qb, QbB = pbf, pbfB, ptb, ptbB
    P1, P2 = rq, rqx
    P1B = [retB, retB]
    P2B = [retB, retB]
    pcol = sb("pcol", [128, 2, 4], F32)
    pcolB = Buf("pcol")
    vP = sb("vP", [128, 12, 32], F32)
    vPB = Buf("vP")
    bbr = tmpf[0:64]

    def sincos_reduce(e, out_ap, phi_ap, tmp_ap):
        e_ = e

    def rr_ops(phi, tmp, rB_, wB_):
        kb.op('dve', lambda e: e.tensor_scalar(out=tmp, in0=phi, scalar1=1.0 / TWO_PI, scalar2=MAGIC,
                                              op0=ALU.mult, op1=ALU.add), r=rB_, w=wB_)
        kb.op('dve', lambda e: e.tensor_scalar(out=tmp, in0=tmp, scalar1=-MAGIC, scalar2=None, op0=ALU.add),
              r=wB_, w=wB_)
        kb.op('dve', lambda e: e.scalar_tensor_tensor(out=phi, in0=tmp, scalar=-TWO_PI, in1=phi,
                                                     op0=ALU.mult, op1=ALU.add), r=rB_ + wB_, w=rB_)
        kb.op('dve', lambda e: e.tensor_scalar(out=phi, in0=phi, scalar1=-3.14159, scalar2=3.14159,
                                              op0=ALU.max, op1=ALU.min), r=rB_, w=rB_)

    def s5_setup(l):
        W0, W1 = sworkB[0], sworkB[1]
        T0, T1 = tmpB[0], tmpB[1]
        for (idx, src) in ((0, lre_d), (1, lim_d)):
            bi = 0
            kb.dma('sp', xin[:32, 0, 0:64], src[l], w=[xinB[0]])
            kb.dma('sp', xin[:32, 0, 64:128], src[l], w=[xinB[0]])
            b = trbank()
            kb.op('pe', lambda e, b=b, bi=bi: e.transpose(ps[:, b, 0:32], xin[:32, 0, 0:128], ident[:32, :32]),
                  r=[xinB[0], cB], w=[bank[b]])
            kb.op('dve', lambda e, b=b, idx=idx: e.tensor_copy(out=vP[:, idx, :], in_=ps[:, b, 0:32]),
                  r=[bank[b]], w=[vPB])
        kb.dma('sp', vP[:, 2, :], ldt_d[l].partition_broadcast(128), w=[vPB])
        kb.op('act', lambda e: e.activation(out=vP[:, 2, :], in_=vP[:, 2, :], func=AF.Exp), r=[vPB], w=[vPB])
        kb.op('dve', lambda e: e.tensor_tensor(out=vP[:, 3, :], in0=vP[:, 0, :], in1=vP[:, 2, :], op=ALU.mult), r=[vPB], w=[vPB])
        kb.op('dve', lambda e: e.tensor_tensor(out=vP[:, 4, :], in0=vP[:, 1, :], in1=vP[:, 2, :], op=ALU.mult), r=[vPB], w=[vPB])

        def trig_small(dst_s, dst_c, mult):
            for (dst, ph) in ((dst_s, 0.0), (dst_c, float(np.pi / 2))):
                kb.op('dve', lambda e, dst=dst, ph=ph: e.tensor_scalar(out=vP[:, dst, :], in0=vP[:, 4, :], scalar1=float(mult),
                                                                       scalar2=ph, op0=ALU.mult, op1=ALU.add), r=[vPB], w=[vPB])
                rr_ops(vP[:, dst, :], vP[:, 11, :], [vPB], [vPB])
                kb.op('act', lambda e, dst=dst: e.activation(out=vP[:, dst, :], in_=vP[:, dst, :], func=AF.Sin), r=[vPB], w=[vPB])
        trig_small(5, 6, 1.0)
        kb.op('act', lambda e: e.activation(out=vP[:, 7, :], in_=vP[:, 3, :], func=AF.Exp), r=[vPB], w=[vPB])
        kb.op('dve', lambda e: e.tensor_tensor(out=vP[:, 5, :], in0=vP[:, 5, :], in1=vP[:, 7, :], op=ALU.mult), r=[vPB], w=[vPB])
        kb.op('dve', lambda e: e.tensor_tensor(out=vP[:, 6, :], in0=vP[:, 6, :], in1=vP[:, 7, :], op=ALU.mult), r=[vPB], w=[vPB])
        kb.op('dve', lambda e: e.tensor_scalar(out=vP[:, 6, :], in0=vP[:, 6, :], scalar1=-1.0, scalar2=None, op0=ALU.add), r=[vPB], w=[vPB])
        kb.op('dve', lambda e: e.tensor_tensor(out=vP[:, 7, :], in0=vP[:, 0, :], in1=vP[:, 0, :], op=ALU.mult), r=[vPB], w=[vPB])
        kb.op('dve', lambda e: e.tensor_tensor(out=vP[:, 8, :], in0=vP[:, 1, :], in1=vP[:, 1, :], op=ALU.mult), r=[vPB], w=[vPB])
        kb.op('dve', lambda e: e.tensor_tensor(out=vP[:, 7, :], in0=vP[:, 7, :], in1=vP[:, 8, :], op=ALU.add), r=[vPB], w=[vPB])
        kb.op('dve', lambda e: e.reciprocal(out=vP[:, 7, :], in_=vP[:, 7, :]), r=[vPB], w=[vPB])
        kb.op('dve', lambda e: e.tensor_tensor(out=vP[:, 8, :], in0=vP[:, 6, :], in1=vP[:, 0, :], op=ALU.mult), r=[vPB], w=[vPB])
        kb.op('dve', lambda e: e.tensor_tensor(out=vP[:, 10, :], in0=vP[:, 5, :], in1=vP[:, 1, :], op=ALU.mult), r=[vPB], w=[vPB])
        kb.op('dve', lambda e: e.tensor_tensor(out=vP[:, 8, :], in0=vP[:, 8, :], in1=vP[:, 10, :], op=ALU.add), r=[vPB], w=[vPB])
        kb.op('dve', lambda e: e.tensor_tensor(out=vP[:, 8, :], in0=vP[:, 8, :], in1=vP[:, 7, :], op=ALU.mult), r=[vPB], w=[vPB])
        kb.op('dve', lambda e: e.tensor_tensor(out=vP[:, 9, :], in0=vP[:, 5, :], in1=vP[:, 0, :], op=ALU.mult), r=[vPB], w=[vPB])
        kb.op('dve', lambda e: e.tensor_tensor(out=vP[:, 10, :], in0=vP[:, 6, :], in1=vP[:, 1, :], op=ALU.mult), r=[vPB], w=[vPB])
        kb.op('dve', lambda e: e.tensor_tensor(out=vP[:, 9, :], in0=vP[:, 9, :], in1=vP[:, 10, :], op=ALU.subtract), r=[vPB], w=[vPB])
        kb.op('dve', lambda e: e.tensor_tensor(out=vP[:, 9, :], in0=vP[:, 9, :], in1=vP[:, 7, :], op=ALU.mult), r=[vPB], w=[vPB])
        trig_small(5, 6, 128.0)
        kb.op('act', lambda e: e.activation(out=vP[:, 7, :], in_=vP[:, 3, :], func=AF.Exp, scale=128.0), r=[vPB], w=[vPB])
        kb.op('dve', lambda e: e.tensor_tensor(out=a128[:, 0, :], in0=vP[:, 6, :], in1=vP[:, 7, :], op=ALU.mult), r=[vPB], w=[s5B])
        kb.op('dve', lambda e: e.tensor_tensor(out=a128[:, 1, :], in0=vP[:, 5, :], in1=vP[:, 7, :], op=ALU.mult), r=[vPB], w=[s5B])
        for kt in range(4):
            gs = slice(kt * 8, (kt + 1) * 8)
            bi = kt % 2
            kb.dma('sp', xin[:64, 0, 0:128].rearrange("p (g h) -> p g h", g=8), bre_d[l, gs].rearrange("g p h -> p g h"), w=[xinB[0]])
            kb.dma('sp', xin[:64, 0, 128:256].rearrange("p (g h) -> p g h", g=8), bim_d[l, gs].rearrange("g p h -> p g h"), w=[xinB[0]])
            BR = xin[:64, 0, 0:128].rearrange("p (g h) -> p g h", g=8)
            BI = xin[:64, 0, 128:256].rearrange("p (g h) -> p g h", g=8)
            CRc = vP[:64, 8, gs].unsqueeze(2).to_broadcast([64, 8, 16])
            CIc = vP[:64, 9, gs].unsqueeze(2).to_broadcast([64, 8, 16])
            o_r = bbr[:, 0, 0:128].rearrange("p (g h) -> p g h", g=8)
            o_i = bbr[:, 0, 128:256].rearrange("p (g h) -> p g h", g=8)
            t_ = bbr[:, 1, 0:128].rearrange("p (g h) -> p g h", g=8)
            kb.op('dve', lambda e: e.tensor_tensor(out=o_r, in0=BR, in1=CRc, op=ALU.mult), r=[xinB[0], vPB], w=[*tmpB])
            kb.op('dve', lambda e: e.tensor_tensor(out=t_, in0=BI, in1=CIc, op=ALU.mult), r=[xinB[0], vPB], w=[*tmpB])
            kb.op('dve', lambda e: e.tensor_tensor(out=o_r, in0=o_r, in1=t_, op=ALU.subtract), r=[*tmpB], w=[*tmpB])
            kb.op('dve', lambda e: e.tensor_tensor(out=o_i, in0=BI, in1=CRc, op=ALU.mult), r=[xinB[0], vPB], w=[*tmpB])
            kb.op('dve', lambda e: e.tensor_tensor(out=t_, in0=BR, in1=CIc, op=ALU.mult), r=[xinB[0], vPB], w=[*tmpB])
            kb.op('dve', lambda e: e.tensor_tensor(out=o_i, in0=o_i, in1=t_, op=ALU.add), r=[*tmpB], w=[*tmpB])
            b = trbank()
            for s_ in range(2):
                kb.op('pe', lambda e, s_=s_, b=b: e.transpose(ps[:, b, s_ * 64:(s_ + 1) * 64], bbr[:, 0, s_ * 128:(s_ + 1) * 128],
                                                              ident[:64, :64]), r=[*tmpB, cB], w=[bank[b]])
            Tr = ps[:, b, 0:64].unsqueeze(1).to_broadcast([128, 8, 64])
            Ti = ps[:, b, 64:128].unsqueeze(1).to_broadcast([128, 8, 64])
            Mk = bmask.unsqueeze(2).to_broadcast([128, 8, 64])
            Bv = Bblk[:, kt, :].rearrange("p (j s q) -> p j s q", j=8, s=2)
            kb.op('dve', lambda e: e.tensor_tensor(out=Bv[:, :, 0, :], in0=Tr, in1=Mk, op=ALU.mult), r=[bank[b], cB], w=[s5B])
            kb.op('dve', lambda e: e.tensor_tensor(out=Bv[:, :, 1, :], in0=Ti, in1=Mk, op=ALU.mult), r=[bank[b], cB], w=[s5B])
        for kt in range(4):
            for (which, dstC) in ((0, C1), (1, C2)):
                bi = (kt + which) % 2
                a_, b_ = (cre_d, cim_d) if which == 0 else (cim_d, cre_d)
                kb.dma('sp', xin[:, 0, 0:64], a_[l, kt * 128:(kt + 1) * 128, :], w=[xinB[0]])
                kb.dma('sp', xin[:, 0, 64:128], b_[l, kt * 128:(kt + 1) * 128, :], w=[xinB[0]])
                b = trbank()
                kb.op('pe', lambda e, b=b, bi=bi: e.transpose(ps[:, b, 0:128], xin[:, 0, 0:128], ident[:]),
                      r=[xinB[0], cB], w=[bank[b]])
                if which == 0:
                    kb.op('dve', lambda e, b=b, kt=kt: e.tensor_scalar(out=C1[:, kt * 128:(kt + 1) * 128], in0=ps[:, b, 0:128],
                                                                       scalar1=sgn1, scalar2=None, op0=ALU.mult), r=[bank[b], cB], w=[s5B])
                else:
                    kb.op('dve', lambda e, b=b, kt=kt: e.tensor_scalar(out=C2[:, kt * 128:(kt + 1) * 128], in0=ps[:, b, 0:128],
                                                                       scalar1=-1.0, scalar2=None, op0=ALU.mult), r=[bank[b]], w=[s5B])
        load_fm(pcol[:, 0, :], sd_d[l], 4, pcolB)
        for kt in range(4):
            kb.op('dve', lambda e, kt=kt: e.tensor_scalar(out=Dd[:, kt, :], in0=ident[:], scalar1=pcol[:, 0, kt:kt + 1], scalar2=None,
                                                          op0=ALU.mult), r=[pcolB, cB], w=[s5B])
        for q in range(8):
            gs = slice(q * 4, (q + 1) * 4)
            TR = trow.unsqueeze(1).to_broadcast([128, 4, 128])
            TH = vP[:, 4, gs].unsqueeze(2).to_broadcast([128, 4, 128])
            ER = vP[:, 3, gs].unsqueeze(2).to_broadcast([128, 4, 128])
            ph = swork[:, 0, :].rearrange("p (g t) -> p g t", g=4)
            mg = tmpf[:, 0, 0:512].rearrange("p (g t) -> p g t", g=4) if T >= 512 else None
            kb.op('dve', lambda e: e.tensor_tensor(out=mg, in0=TR, in1=ER, op=ALU.mult), r=[vPB, cB], w=[T0])
            kb.op('act', lambda e: e.activation(out=tmpf[:, 0, 0:512], in_=tmpf[:, 0, 0:512], func=AF.Exp), r=[T0], w=[T0])
            for (dst, phs) in ((AT2, 0.0), (AT1, float(np.pi / 2))):
                kb.op('dve', lambda e: e.tensor_tensor(out=ph, in0=TR, in1=TH, op=ALU.mult), r=[vPB, cB], w=[W0])
                if phs != 0.0:
                    kb.op('dve', lambda e: e.tensor_scalar(out=swork[:, 0, :], in0=swork[:, 0, :], scalar1=phs, scalar2=None,
                                                          op0=ALU.add), r=[W0], w=[W0])
                rr_ops(swork[:, 0, :], swork[:, 1, :], [W0], [W1])
                kb.op('act', lambda e: e.activation(out=swork[:, 0, :], in_=swork[:, 0, :], func=AF.Sin), r=[W0], w=[W0])
                kb.op('dve', lambda e, dst=dst: e.tensor_tensor(out=dst[:, gs, :].rearrange("p g t -> p (g t)"), in0=swork[:, 0, :],
                                                                in1=tmpf[:, 0, 0:512], op=ALU.mult), r=[W0, T0], w=[s5B])
        for q in range(4):
            gs = slice(q * 8, (q + 1) * 8)
            kb.dma('sp', swork[:, 1, :], lre_d[l, gs].rearrange("g p -> (g p)").partition_broadcast(128), w=[W1])
            kb.dma('sp', tmpf[:, 1, 0:512], lim_d[l, gs].rearrange("g p -> (g p)").partition_broadcast(128), w=[T1])
            DTq = vP[:, 2, gs].unsqueeze(2).to_broadcast([128, 8, 64])
            erq = swork[:, 1, :].rearrange("p (g q) -> p g q", g=8)
            thq = tmpf[:, 1, 0:512].rearrange("p (g q) -> p g q", g=8)
            kb.op('dve', lambda e: e.tensor_tensor(out=erq, in0=erq, in1=DTq, op=ALU.mult), r=[W1, vPB], w=[W1])
            kb.op('dve', lambda e: e.tensor_tensor(out=thq, in0=thq, in1=DTq, op=ALU.mult), r=[T1, vPB], w=[T1])
            kb.op('act', lambda e: e.activation(out=swork[:, 1, :], in_=swork[:, 1, :], func=AF.Exp, scale=negmcol), r=[W1, cB], w=[W1])
            for (dst, phs, sg) in ((ciT, 0.0, -1.0), (crT, float(np.pi / 2), 1.0)):
                kb.op('dve', lambda e, phs=phs: e.tensor_scalar(out=swork[:, 0, :], in0=tmpf[:, 1, 0:512], scalar1=mcol, scalar2=phs,
                                                                op0=ALU.mult, op1=ALU.add), r=[T1, cB], w=[W0])
                rr_ops(swork[:, 0, :], tmpf[:, 0, 0:512], [W0], [T0])
                kb.op('act', lambda e: e.activation(out=swork[:, 0, :], in_=swork[:, 0, :], func=AF.Sin), r=[W0], w=[W0])
                kb.op('dve', lambda e, dst=dst, sg=sg: e.scalar_tensor_tensor(
                    out=dst[:, gs, :].rearrange("p g q -> p (g q)"), in0=swork[:, 0, :], scalar=sg, in1=swork[:, 1, :],
                    op0=ALU.mult, op1=ALU.mult), r=[W0, W1], w=[s5B])
        kb.op('dve', lambda e: e.memset(Scol[:], 0.0), w=[SrB])

    def s5(l, g):
        for c in range(NCH):
            cs = slice(c * CH, (c + 1) * CH)
            yb = atbank()
            sb_ = atbank()
            for kt in range(4):
                kb.op('pe', lambda e, kt=kt, yb=yb: e.matmul(ps[:, yb, kt * 128:(kt + 1) * 128], uT[:, kt, cs], Dd[:, kt, :],
                                                             start=True, stop=False), r=[uB, s5B], w=[bank[yb]])
                for hq in range(2):
                    g0 = kt * 8 + hq * 4
                    j = hq
                    bA = mmbank()
                    kb.op('pe', lambda e: e.matmul(ps[:, bA, :], uT[:, kt, cs], Bblk[:, kt, hq * 512:(hq + 1) * 512], start=True, stop=True),
                          r=[uB, s5B], w=[bank[bA]])
                    kb.op('dve', lambda e: e.tensor_tensor(
                        out=Qa[:, j, :].rearrange("p (g s q) -> p g s q", g=4, s=2),
                        in0=ps[:, bA, :].rearrange("p (g s q) -> p g s q", g=4, s=2),
                        in1=crT[:, g0:g0 + 4, :].unsqueeze(2).to_broadcast([128, 4, 2, 64]), op=ALU.mult),
                        r=[bank[bA], s5B], w=[QaB[j]])
                    bv_ = ps[:, bA, :].rearrange("p (g s q) -> p g s q", g=4, s=2)
                    qv_ = Qb[:, j, :].rearrange("p (g s q) -> p g s q", g=4, s=2)
                    kb.op('dve', lambda e: e.scalar_tensor_tensor(out=qv_[:, :, 0, :], in0=bv_[:, :, 1, :], scalar=-1.0,
                                                                 in1=ciT[:, g0:g0 + 4, :], op0=ALU.mult, op1=ALU.mult),
                          r=[bank[bA], s5B], w=[QbB[j]])
                    kb.op('dve', lambda e: e.tensor_tensor(out=qv_[:, :, 1, :], in0=bv_[:, :, 0, :], in1=ciT[:, g0:g0 + 4, :], op=ALU.mult),
                          r=[bank[bA], s5B], w=[QbB[j]])
                    wbk = mmbank()
                    for gg in range(4):
                        gi = g0 + gg
                        o_ = ps[:, wbk, gg * 128:(gg + 1) * 128]
                        kb.op('pe', lambda e: e.matmul(o_, Qa[:, j, gg * 128:(gg + 1) * 128], trib[:], start=True, stop=False),
                              r=[QaB[j], cB], w=[bank[wbk]])
                        kb.op('pe', lambda e: e.matmul(o_, Qb[:, j, gg * 128:(gg + 1) * 128], trib[:], start=False, stop=True),
                              r=[QbB[j], cB], w=[bank[wbk]])
                    for gg in range(4):
                        gi = g0 + gg
                        kb.op('dve', lambda e: e.scalar_tensor_tensor(
                            out=P1[:, j, gg * 128:(gg + 1) * 128], in0=ps[:, wbk, gg * 128:(gg + 1) * 128],
                            scalar=Scol[:, gi:gi + 1], in1=AT1[:, gi, :], op0=ALU.add, op1=ALU.mult),
                            r=[bank[wbk], s5B, SrB], w=[P1B[j]])
                        kb.op('dve', lambda e: e.scalar_tensor_tensor(
                            out=P2[:, j, gg * 128:(gg + 1) * 128], in0=ps[:, wbk, gg * 128:(gg + 1) * 128],
                            scalar=Scol[:, gi:gi + 1], in1=AT2[:, gi, :], op0=ALU.add, op1=ALU.mult),
                            r=[bank[wbk], s5B, SrB], w=[P2B[j]])
                    w127 = ps[:, wbk, :].rearrange("p (g t) -> p g t", g=4)[:, :, 127]
                    kb.op('dve', lambda e: e.tensor_tensor(out=pcol[:, 1, :], in0=w127, in1=Scol[:, g0:g0 + 4], op=ALU.add),
                          r=[bank[wbk], SrB], w=[pcolB])
                    kb.op('dve', lambda e: e.tensor_tensor(out=pcol[:, 0, :], in0=pcol[:, 1, :], in1=a128[:, 0, g0:g0 + 4], op=ALU.mult),
                          r=[s5B, pcolB], w=[pcolB])
                    kb.op('dve', lambda e: e.tensor_tensor(out=pcol[:, 1, :], in0=pcol[:, 1, :], in1=a128[:, 1, g0:g0 + 4], op=ALU.mult),
                          r=[s5B, pcolB], w=[pcolB])
                    kb.op('pe', lambda e: e.matmul(ps[:, sb_, 0:4], ident[:], pcol[:, 0, :], start=True, stop=False),
                          r=[pcolB, cB], w=[bank[sb_]])
                    kb.op('pe', lambda e: e.matmul(ps[:, sb_, 0:4], Jm, pcol[:, 1, :], start=False, stop=True),
                          r=[pcolB, cB], w=[bank[sb_]])
                    kb.op('act', lambda e: e.activation(out=Scol[:, g0:g0 + 4], in_=ps[:, sb_, 0:4], func=AF.Copy),
                          r=[bank[sb_]], w=[SrB])
                    for gg in range(4):
                        gi = g0 + gg
                        last = (hq == 1 and gg == 3)
                        o_ = ps[:, yb, gi * 16:(gi + 1) * 16]
                        kb.op('pe', lambda e: e.matmul(o_, P1[:, j, gg * 128:(gg + 1) * 128], C1[:, gi * 16:(gi + 1) * 16],
                                                       start=False, stop=False), r=[P1B[j], s5B], w=[bank[yb]])
                        kb.op('pe', lambda e: e.matmul(o_, P2[:, j, gg * 128:(gg + 1) * 128], C2[:, gi * 16:(gi + 1) * 16],
                                                       start=False, stop=last), r=[P2B[j], s5B], w=[bank[yb]])
            kb.op('act', lambda e: e.activation(out=osb[:], in_=ps[:, yb, :], func=AF.Gelu_apprx_tanh), r=[bank[yb]], w=[osbB])
            tb = trbank()
            for j in range(4):
                kb.op('pe', lambda e, j=j, tb=tb: e.transpose(psbf(tb)[:, j * 128:(j + 1) * 128], osb[:, j * 128:(j + 1) * 128], identb[:]),
                      r=[osbB, cB], w=[bank[tb]])
            kb.op('dve', lambda e, tb=tb: e.tensor_copy(out=zT[:, :, cs], in_=psbf(tb)[:, 0:512].rearrange("p (j t) -> p j t", j=4)),
                  r=[bank[tb]], w=[zB])
        for nt in range(4):
            b = mmbank()
            for kt in range(4):
                kb.op('pe', lambda e, kt=kt, nt=nt, b=b: e.matmul(ps[:, b, 0:T], glw[:, kt, nt * 128:(nt + 1) * 128], zT[:, kt, :],
                                                                  start=(kt == 0), stop=(kt == 3)), r=[lB, zB], w=[bank[b]])
            kb.op('act', lambda e, nt=nt, b=b: e.activation(out=swork[:, 0, 0:T], in_=ps[:, b, 0:T], func=AF.Sigmoid,
                                                           bias=glb[:, l, nt:nt + 1], scale=1.0), r=[bank[b], lB], w=[sworkB[0]])
            kb.op('dve', lambda e, nt=nt: e.tensor_tensor(out=ycT[:, nt, :], in0=zT[:, nt, :], in1=swork[:, 0, 0:T], op=ALU.mult),
                  r=[zB, sworkB[0]], w=[yB[nt]])

    outev = []
    hsB = [Buf("hs%d" % g) for g in range(NG)]
    for l in range(depth):
        if mixers:
            layer_setup(l)
            s5_setup(l)
        for g in range(NG):
            if l == 0:
                load_x_group(g)
            else:
                kb.dma('pool', hT[:].rearrange("p k t -> p (k t)"), h_s[g], r=[hsB[g]], w=hB)
            if dbg not in ('io', 'cast', 'ada'):
                norm_mod(l, 0)
                if mixers:
                    in_proj(l, g)
                    swa(l, g)
                    ret(l, g)
                    s5(l, g)
                    mla(l, g)
                    if dbgy_d is not None and l == depth - 1:
                        kb.dma('pool', dbgy_d[g], ycT[:].rearrange("p k t -> p (k t)"), r=yB)
                    out_proj(l)
                norm_mod(l, 3)
                if dbg != 'norm':
                    mlp(l)
            if l == depth - 1:
                final_out_group(g, l)
            else:
                kb.dma('pool', h_s[g], hT[:].rearrange("p k t -> p (k t)"), r=hB, w=[hsB[g]])
    kb._wait('sp', outev)
    kb.barrier()
    es.close()
    return nc, kb


def _const_tables(S):
    f32 = np.float32
    inv = (np.float32(10000.0) ** (-(np.arange(0, 64, 2, dtype=f32) / f32(64)))).astype(f32)
    ang = (np.arange(S, dtype=f32)[None, :] * inv[:, None]).astype(f32).astype(np.float64)
    p = np.arange(128)
    fi = p % 32
    half = (p % 64) // 32
    rotC = np.cos(ang)[fi].astype(f32)
    rotS = (np.sin(ang)[fi] * np.where(half == 0, -1.0, 1.0)[:, None]).astype(f32)
    r = np.arange(128)[:, None]
    j = np.arange(256)[None, :]
    dist = r + 128 - j
    valid = (dist >= 0) & (dist < 128)
    mA = np.where(valid, 0.0, -240000.0).astype(f32)
    mB = mA.copy()
    mB[:, :128] = -240000.0
    swa_mask = np.stack([mA, mB], 0)
    mla_mask = np.where(np.arange(128)[None, :] <= np.arange(128)[:, None], 0.0, -30000.0).astype(f32)
    lg = np.log1p(-(2.0 ** (-5.0 - np.arange(4, dtype=np.float64))))
    m = np.arange(128)[:, None]
    c = np.arange(128)[None, :]
    decT = np.zeros((128, 4, 128), np.float64)
    for h in range(4):
        decT[:, h, :] = np.where(c >= m, np.exp(lg[h] * np.maximum(c - m, 0)), 0.0) / 8.0
    xi = np.zeros((128, 2, 512), np.float64)
    for pr in range(2):
        for hh in range(2):
            h = 2 * pr + hh
            xi[64 * hh:64 * hh + 64, pr, :] = np.exp(lg[h] * ((np.arange(512) % 128) + 1.0))[None, :]
    zeta = np.zeros((128, 4, 64), np.float64)
    for h in range(4):
        zeta[:, h, :] = (np.exp(lg[h] * (127.0 - np.arange(128))) / 8.0)[:, None]
    s5c = np.zeros((128, 640), f32)
    s5c[:, 0:128] = (np.arange(128)[:, None] <= np.arange(128)[None, :])
    J = np.zeros((128, 128), f32)
    for pp in range(64):
        J[pp + 64, pp] = -1.0
        J[pp, pp + 64] = 1.0
    s5c[:, 128:256] = J
    s5c[:, 256:264] = (np.arange(128)[:, None] // 16 == np.arange(8)[None, :])
    s5c[:, 264] = np.arange(128)
    s5c[:, 265] = -np.arange(128)
    s5c[:, 266] = np.where(np.arange(128) < 64, 1.0, -1.0)
    s5c[:, 384:512] = 1.0
    s5c[:, 512:640] = np.arange(128)[None, :]
    return {
        "s5_consts": s5c,
        "rotC": rotC, "rotS": rotS, "swa_mask": swa_mask, "mla_mask": mla_mask,
        "ret_decT": decT.reshape(128, 512).astype(f32), "ret_xi": xi.reshape(128, 1024).astype(f32),
        "ret_zeta": zeta.reshape(128, 256).astype(f32), "ident": np.eye(128, dtype=f32),
    }


def host_inputs(inputs, b, S=4096):
    f = lambda a: np.ascontiguousarray(a, dtype=np.float32)
    uq = inputs["mla_w_uq"]
    cols = []
    for h in range(4):
        cols += list(range(h * 192, h * 192 + 128))
    for h in range(4):
        cols += list(range(h * 192 + 128, h * 192 + 192))
    for h in range(4):
        cols += list(range(h * 192 + 160, h * 192 + 192)) + list(range(h * 192 + 128, h * 192 + 160))
    ukv = inputs["mla_w_ukv"]
    ukT = np.stack([np.concatenate([ukv[l][:, h * 256:h * 256 + 128].T for h in range(4)], axis=1) for l in range(2)])
    uv = np.stack([np.concatenate([ukv[l][:, h * 256 + 128:h * 256 + 256] for h in range(4)], axis=1) for l in range(2)])
    m = {
        "x": f(inputs["x"][b][:S]),
        "c": f(inputs["c"][b].reshape(16, 128)),
        "norm1_g": f(inputs["norm1_g"].reshape(2, 16, 128)),
        "norm2_g": f(inputs["norm2_g"].reshape(2, 16, 128)),
        "ada_w": f(inputs["ada_w"]),
        "ada_b": f(inputs["ada_b"].reshape(2, 96, 128)),
        "w_in_ext": f(inputs["w_in"][:, :, INPERM]),
        "w_out": f(inputs["w_out"]),
        "mlp_w1": f(inputs["mlp_w1"]),
        "mlp_w2": f(inputs["mlp_w2"]),
        "final_norm_g": f(inputs["final_norm_g"].reshape(16, 128)),
        "swa_sinks": f(inputs["swa_sinks"]),
        "mla_q_norm": f(inputs["mla_q_norm"].reshape(2, 3, 128)),
        "mla_kv_norm": f(inputs["mla_kv_norm"].reshape(2, 1, 128)),
        "w_uq_ext": f(uq[:, :, np.array(cols)]),
        "w_ukT": f(ukT),
        "w_uv": f(uv),
        "s5_glu_w": f(inputs["s5_glu_w"]),
        "s5_glu_b": f(inputs["s5_glu_b"].reshape(2, 4, 128)),
        "s5_lambda_re": f(inputs["s5_lambda_re"]),
        "s5_lambda_im": f(inputs["s5_lambda_im"]),
        "s5_log_dt": f(inputs["s5_log_dt"].reshape(2, 1, 32)),
        "s5_b_re": f(inputs["s5_b_re"]),
        "s5_b_im": f(inputs["s5_b_im"]),
        "s5_c_re": f(inputs["s5_c_re"].reshape(2, 512, 64)),
        "s5_c_im": f(inputs["s5_c_im"].reshape(2, 512, 64)),
        "s5_d": f(inputs["s5_d"].reshape(2, 4, 128)),
    }
    m.update(_const_tables(S))
    return m


_CACHE = {}


def kernel(**inputs):
    if 'nc' not in _CACHE:
        _CACHE['nc'] = build_program()[0]
    nc = _CACHE['nc']
    B = inputs["x"].shape[0]
    shared = host_inputs(inputs, 0)
    in_maps = []
    for b in range(B):
        m = dict(shared)
        m["x"] = np.ascontiguousarray(inputs["x"][b], dtype=np.float32)
        m["c"] = np.ascontiguousarray(np.asarray(inputs["c"][b]).reshape(16, 128), dtype=np.float32)
        in_maps.append(m)
    res = run_bass_kernel_spmd(nc, in_maps, core_ids=list(range(B)))
    out = np.stack([np.asarray(res.results[b]["out"]) for b in range(B)], axis=0)
    return out.astype(np.float32)
```
